# Optimizing a Trainium2 kernel written in Bass

```python
import jax, jax.numpy as jnp
from jax import lax
import numpy as np

D_MODEL = 2048
BATCH = 8
SEQ = 2048
DEPTH = 2

N_MIXERS = 4
GROUP_WIDTH = D_MODEL // N_MIXERS
HEAD_DIM = 128
N_HEADS = GROUP_WIDTH // HEAD_DIM
ROT_DIM = HEAD_DIM // 4
ROPE_THETA = 500000.0

NSA_KV_BRANCHES = 3
CMP_BLOCK = 32
CMP_STRIDE = 16
CMP_HIDDEN = 128
SEL_BLOCK = 64
SEL_TOPK = 8
WIN = 256
WIN_BLOCK = 128
NSA_Q_CHUNK = 128
FORCE_SCORE = 1e9

MLSTM_CHUNK = 64

CONV_WIDTH = 4
LRU_C = 8.0
LRU_BLOCKS = N_HEADS
LRU_BLOCK_WIDTH = GROUP_WIDTH // LRU_BLOCKS

MOBA_BLOCK = 256
MOBA_TOPK = 3
MOBA_Q_CHUNK = 32

DEEPNORM_ALPHA = (2 * DEPTH) ** 0.25
DEEPNORM_BETA = (8 * DEPTH) ** -0.25
NEG = -1e30
LN_EPS = 1e-5

IN_SPLITS = (
    ('nsa_q', GROUP_WIDTH),
    ('nsa_kv', 2 * NSA_KV_BRANCHES * HEAD_DIM),
    ('nsa_gate', NSA_KV_BRANCHES * N_HEADS),
    ('nsa_z', GROUP_WIDTH),
    ('mlstm_qkv', 3 * GROUP_WIDTH),
    ('mlstm_if', 2 * N_HEADS),
    ('mlstm_o', GROUP_WIDTH),
    ('mlstm_z', GROUP_WIDTH),
    ('lru_x', GROUP_WIDTH),
    ('lru_z', GROUP_WIDTH),
    ('moba_qkv', 3 * GROUP_WIDTH),
    ('moba_z', GROUP_WIDTH),
)
IN_WIDTH = sum(w for _, w in IN_SPLITS)
MIX_WIDTH = N_MIXERS * GROUP_WIDTH

kernel_name = 'hybrid_nsa_mlstm_rglru_moba_deepnorm'


def split_columns(h):
    parts, start = {}, 0
    for name, width in IN_SPLITS:
        parts[name] = h[..., start:start + width]
        start += width
    return parts


def layer_norm(x, g, b):
    xf = x.astype(jnp.float32)
    mu = jnp.mean(xf, -1, keepdims=True)
    var = jnp.mean(jnp.square(xf - mu), -1, keepdims=True)
    y = (xf - mu) * lax.rsqrt(var + LN_EPS) * g.astype(jnp.float32) + b.astype(jnp.float32)
    return y.astype(x.dtype)


def partial_rotary(t, pos):
    half = ROT_DIM // 2
    inv_freq = jnp.power(ROPE_THETA, -jnp.arange(half, dtype=jnp.float32) * (2.0 / ROT_DIM))
    ang = pos.astype(jnp.float32)[:, None] * inv_freq[None, :]
    cos = jnp.cos(ang)[:, None, :]
    sin = jnp.sin(ang)[:, None, :]
    tr = t[..., :ROT_DIM].astype(jnp.float32)
    t1, t2 = tr[..., :half], tr[..., half:]
    rot = jnp.concatenate([t1 * cos - t2 * sin, t2 * cos + t1 * sin], axis=-1)
    return jnp.concatenate([rot.astype(t.dtype), t[..., ROT_DIM:]], axis=-1)


def masked_softmax(s, mask):
    s = jnp.where(mask, s.astype(jnp.float32), NEG)
    return jax.nn.softmax(s, axis=-1) * mask


def nsa_mixer(q, k_cmp, v_cmp, k_sel, v_sel, k_win, v_win, gates, cmp_w1, cmp_w2, cmp_pe):
    B, S, H, dh = q.shape
    scale = dh ** -0.5
    f32 = jnp.float32

    n_cmp = (S - CMP_BLOCK) // CMP_STRIDE + 1
    cmp_start = jnp.arange(n_cmp) * CMP_STRIDE
    idx = cmp_start[:, None] + jnp.arange(CMP_BLOCK)[None, :]

    def compress(t, w1, w2, pe):
        blocks = t[:, idx] + pe
        hid = jax.nn.silu(blocks.reshape(B, n_cmp, CMP_BLOCK * dh) @ w1)
        return hid @ w2

    kc = compress(k_cmp, cmp_w1[0], cmp_w2[0], cmp_pe[0])
    vc = compress(v_cmp, cmp_w1[1], cmp_w2[1], cmp_pe[1])
    cmp_end = cmp_start + CMP_BLOCK - 1

    n_sel = S // SEL_BLOCK
    sel_start = jnp.arange(n_sel) * SEL_BLOCK
    overlap = ((cmp_start[:, None] < sel_start[None, :] + SEL_BLOCK)
               & (cmp_start[:, None] + CMP_BLOCK > sel_start[None, :])).astype(f32)
    k_sel_blocks = k_sel.reshape(B, n_sel, SEL_BLOCK, dh)
    v_sel_blocks = v_sel.reshape(B, n_sel, SEL_BLOCK, dh)
    top_n = min(SEL_TOPK, n_sel)
    sel_ids = jnp.arange(n_sel)
    b_idx = jnp.arange(B)[:, None, None]

    QC = NSA_Q_CHUNK
    n_chunks = S // QC
    q_chunks = q.reshape(B, n_chunks, QC, H, dh).transpose(1, 0, 2, 3, 4)

    def chunk_fn(args):
        qc, c = args
        t = c * QC + jnp.arange(QC)
        s_c = jnp.einsum('bqhd,bnd->bhqn', qc, kc) * scale
        p_c = masked_softmax(s_c, cmp_end[None, :] <= t[:, None])
        o_c = jnp.einsum('bhqn,bnd->bqhd', p_c.astype(vc.dtype), vc)
        imp = jnp.einsum('bhqn,nj->bqj', p_c, overlap)
        cur = t // SEL_BLOCK
        forced = (sel_ids[None] == 0) | (sel_ids[None] == cur[:, None]) | (sel_ids[None] == cur[:, None] - 1)
        valid = sel_start[None] <= t[:, None]
        imp = jnp.where(forced, FORCE_SCORE, jnp.where(valid, imp, NEG))
        _, sel = lax.top_k(imp, top_n)
        ks = k_sel_blocks[b_idx, sel]
        vs = v_sel_blocks[b_idx, sel]
        s_s = jnp.einsum('bqhd,bqnkd->bhqnk', qc, ks).reshape(B, H, QC, top_n * SEL_BLOCK) * scale
        key_pos = sel[..., None] * SEL_BLOCK + jnp.arange(SEL_BLOCK)
        mask_s = (key_pos <= t[None, :, None, None]).reshape(B, 1, QC, top_n * SEL_BLOCK)
        p_s = masked_softmax(s_s, mask_s)
        o_s = jnp.einsum('bhqm,bqmd->bqhd', p_s.astype(vs.dtype), vs.reshape(B, QC, top_n * SEL_BLOCK, dh))
        return o_c, o_s

    o_c, o_s = lax.map(chunk_fn, (q_chunks, jnp.arange(n_chunks)))
    o_c = o_c.transpose(1, 0, 2, 3, 4).reshape(B, S, H, dh)
    o_s = o_s.transpose(1, 0, 2, 3, 4).reshape(B, S, H, dh)

    nwb = S // WIN_BLOCK
    n_prev = WIN // WIN_BLOCK

    def banded(t):
        tb = t.reshape(B, nwb, WIN_BLOCK, dh)
        shifted = [jnp.pad(tb, ((0, 0), (s, 0), (0, 0), (0, 0)))[:, :nwb] for s in range(n_prev, -1, -1)]
        return jnp.concatenate(shifted, axis=2)

    kw, vw = banded(k_win), banded(v_win)
    qb = q.reshape(B, nwb, WIN_BLOCK, H, dh)
    s_w = jnp.einsum('bnqhd,bnkd->bhnqk', qb, kw) * scale
    qpos = jnp.arange(S).reshape(nwb, WIN_BLOCK)
    kpos = qpos[:, :1] - n_prev * WIN_BLOCK + jnp.arange((n_prev + 1) * WIN_BLOCK)[None, :]
    diff = qpos[:, :, None] - kpos[:, None, :]
    mask_w = (kpos[:, None, :] >= 0) & (diff >= 0) & (diff < WIN)
    p_w = masked_softmax(s_w, mask_w)
    o_w = jnp.einsum('bhnqk,bnkd->bnqhd', p_w.astype(vw.dtype), vw).reshape(B, S, H, dh)

    o = gates[..., 0:1] * o_c + gates[..., 1:2] * o_s + gates[..., 2:3] * o_w
    return o.reshape(B, S, H * dh)


def mlstm_mixer(q, k, v, i_pre, f_pre, o_pre, norm_g):
    B, S, H, dh = q.shape
    L = MLSTM_CHUNK
    nc = S // L
    f32 = jnp.float32

    def heads_to_chunks(t):
        return t.astype(f32).reshape(B, nc, L, H, dh).transpose(0, 3, 1, 2, 4)

    def gate_to_chunks(t):
        return t.astype(f32).reshape(B, nc, L, H).transpose(0, 3, 1, 2)

    qc = heads_to_chunks(q)
    kc = heads_to_chunks(k) * (dh ** -0.5)
    vc = heads_to_chunks(v)
    ig = gate_to_chunks(i_pre)
    log_f = jax.nn.log_sigmoid(gate_to_chunks(f_pre))
    b = jnp.cumsum(log_f, axis=-1)
    causal = jnp.tril(jnp.ones((L, L), dtype=bool))
    d_log = jnp.where(causal, b[..., :, None] - b[..., None, :] + ig[..., None, :], NEG)
    m_intra = jnp.max(d_log, axis=-1)

    b_last = b[..., -1]
    w_log = b_last[..., None] - b + ig
    m_loc = jnp.max(w_log, axis=-1)
    e = jnp.exp(w_log - m_loc[..., None])
    g_c = jnp.einsum('bhcld,bhcle->bhcde', e[..., None] * kc, vc)
    g_n = jnp.einsum('bhcl,bhcld->bhcd', e, kc)

    def step(carry, xs):
        c_st, n_st, m_st = carry
        a, ml, gc, gn = xs
        m_new = jnp.maximum(a + m_st, ml)
        s_old = jnp.exp(a + m_st - m_new)
        s_new = jnp.exp(ml - m_new)
        c_next = s_old[..., None, None] * c_st + s_new[..., None, None] * gc
        n_next = s_old[..., None] * n_st + s_new[..., None] * gn
        return (c_next, n_next, m_new), (c_st, n_st, m_st)

    init = (jnp.zeros((B, H, dh, dh), f32), jnp.zeros((B, H, dh), f32), jnp.full((B, H), NEG, f32))
    xs = (jnp.moveaxis(b_last, -1, 0), jnp.moveaxis(m_loc, -1, 0),
          jnp.moveaxis(g_c, 2, 0), jnp.moveaxis(g_n, 2, 0))
    _, (c_prev, n_prev, m_prev) = lax.scan(step, init, xs)
    c_prev = jnp.moveaxis(c_prev, 0, 2)
    n_prev = jnp.moveaxis(n_prev, 0, 2)
    m_prev = jnp.moveaxis(m_prev, 0, -1)

    m_inter = b + m_prev[..., None]
    m_t = jnp.maximum(m_inter, m_intra)
    w_inter = jnp.exp(m_inter - m_t)
    qk = jnp.einsum('bhcld,bhcsd->bhcls', qc, kc) * jnp.exp(d_log - m_t[..., None])
    num = (jnp.einsum('bhcls,bhcse->bhcle', qk, vc)
           + w_inter[..., None] * jnp.einsum('bhcld,bhcde->bhcle', qc, c_prev))
    den = jnp.sum(qk, axis=-1) + w_inter * jnp.einsum('bhcld,bhcd->bhcl', qc, n_prev)
    h = num / jnp.maximum(jnp.abs(den), jnp.exp(-m_t))[..., None]
    h = h.transpose(0, 2, 3, 1, 4).reshape(B, S, H, dh)
    h = jax.nn.sigmoid(o_pre.astype(f32)).reshape(B, S, H, dh) * h
    mu = jnp.mean(h, -1, keepdims=True)
    var = jnp.mean(jnp.square(h - mu), -1, keepdims=True)
    return ((h - mu) * lax.rsqrt(var + LN_EPS)).reshape(B, S, H * dh) * norm_g.astype(f32)


def rglru_mixer(xr, conv_w, conv_b, gate_w, gate_b, lam):
    B, S, W = xr.shape
    f32 = jnp.float32
    u = lax.conv_general_dilated(xr, conv_w[:, None, :], window_strides=(1,),
                                 padding=[(CONV_WIDTH - 1, 0)],
                                 dimension_numbers=('NWC', 'WIO', 'NWC'),
                                 feature_group_count=W) + conv_b
    ub = u.reshape(B, S, LRU_BLOCKS, LRU_BLOCK_WIDTH)
    r = jax.nn.sigmoid((jnp.einsum('bsnd,nde->bsne', ub, gate_w[0]).reshape(B, S, W) + gate_b[0]).astype(f32))
    i = jax.nn.sigmoid((jnp.einsum('bsnd,nde->bsne', ub, gate_w[1]).reshape(B, S, W) + gate_b[1]).astype(f32))
    log_a = -LRU_C * r * jax.nn.softplus(-lam.astype(f32))
    a = jnp.exp(log_a)
    bx = jnp.sqrt(-jnp.expm1(2.0 * log_a)) * (i * u.astype(f32))

    def combine(lhs, rhs):
        a1, b1 = lhs
        a2, b2 = rhs
        return a1 * a2, a2 * b1 + b2

    _, h = lax.associative_scan(combine, (a, bx), axis=1)
    return h


def moba_mixer(q, k, v):
    B, S, H, dh = q.shape
    scale = dh ** -0.5
    nb = -(-S // MOBA_BLOCK)
    Sp = nb * MOBA_BLOCK
    pad = ((0, 0), (0, Sp - S), (0, 0), (0, 0))
    qh = jnp.pad(q, pad).transpose(0, 2, 1, 3)
    kb = jnp.pad(k, pad).transpose(0, 2, 1, 3).reshape(B, H, nb, MOBA_BLOCK, dh)
    vb = jnp.pad(v, pad).transpose(0, 2, 1, 3).reshape(B, H, nb, MOBA_BLOCK, dh)
    kmean = jnp.mean(kb.astype(jnp.float32), axis=3).astype(kb.dtype)
    top_n = min(MOBA_TOPK, nb)
    QC = MOBA_Q_CHUNK
    n_chunks = Sp // QC
    q_chunks = qh.reshape(B, H, n_chunks, QC, dh).transpose(2, 0, 1, 3, 4)
    b_idx = jnp.arange(B)[:, None, None, None]
    h_idx = jnp.arange(H)[None, :, None, None]
    blk_ids = jnp.arange(nb)

    def chunk_fn(args):
        qc, c = args
        t = c * QC + jnp.arange(QC)
        cur = (c * QC) // MOBA_BLOCK
        gs = jnp.einsum('bhqd,bhnd->bhqn', qc, kmean).astype(jnp.float32)
        gs = jnp.where(blk_ids < cur, gs, NEG)
        _, sel = lax.top_k(gs, top_n)
        sel_valid = sel < cur
        ks = kb[b_idx, h_idx, sel]
        vs = vb[b_idx, h_idx, sel]
        s_sel = jnp.einsum('bhqd,bhqnkd->bhqnk', qc, ks).reshape(B, H, QC, top_n * MOBA_BLOCK)
        m_sel = jnp.broadcast_to(sel_valid[..., None], (B, H, QC, top_n, MOBA_BLOCK)).reshape(B, H, QC, top_n * MOBA_BLOCK)
        k_own = lax.dynamic_index_in_dim(kb, cur, axis=2, keepdims=False)
        v_own = lax.dynamic_index_in_dim(vb, cur, axis=2, keepdims=False)
        s_own = jnp.einsum('bhqd,bhkd->bhqk', qc, k_own)
        own_pos = cur * MOBA_BLOCK + jnp.arange(MOBA_BLOCK)
        m_own = jnp.broadcast_to(own_pos[None, :] <= t[:, None], (B, H, QC, MOBA_BLOCK))
        s = jnp.concatenate([s_sel, s_own], axis=-1) * scale
        mask = jnp.concatenate([m_sel, m_own], axis=-1)
        p = masked_softmax(s, mask).astype(vb.dtype)
        n_s = top_n * MOBA_BLOCK
        o = (jnp.einsum('bhqm,bhqmd->bhqd', p[..., :n_s], vs.reshape(B, H, QC, n_s, dh))
             + jnp.einsum('bhqk,bhkd->bhqd', p[..., n_s:], v_own))
        return o

    o = lax.map(chunk_fn, (q_chunks, jnp.arange(n_chunks)))
    o = o.transpose(1, 2, 0, 3, 4).reshape(B, H, Sp, dh)[:, :, :S]
    return o.transpose(0, 2, 1, 3).reshape(B, S, H * dh)


def hybrid_layer(x, pos, w_in, nsa_cmp_w1, nsa_cmp_w2, nsa_cmp_pe, mlstm_i_bias, mlstm_f_bias,
                 mlstm_norm_g, lru_conv_w, lru_conv_b, lru_gate_w, lru_gate_b, lru_lambda,
                 w_out, ln_g, ln_b):
    B, S, _ = x.shape
    H, dh = N_HEADS, HEAD_DIM
    parts = split_columns(x @ w_in)

    qa = partial_rotary(parts['nsa_q'].reshape(B, S, H, dh), pos)
    kv = parts['nsa_kv'].reshape(B, S, 2, NSA_KV_BRANCHES, dh)
    ka = partial_rotary(kv[:, :, 0], pos)
    va = kv[:, :, 1]
    gates = jax.nn.sigmoid(parts['nsa_gate'].reshape(B, S, H, NSA_KV_BRANCHES))
    y_a = nsa_mixer(qa, ka[:, :, 0], va[:, :, 0], ka[:, :, 1], va[:, :, 1], ka[:, :, 2], va[:, :, 2],
                    gates, nsa_cmp_w1, nsa_cmp_w2, nsa_cmp_pe)
    y_a = y_a.astype(x.dtype) * jax.nn.silu(parts['nsa_z'])

    qkv_b = parts['mlstm_qkv'].reshape(B, S, 3, H, dh)
    if_b = parts['mlstm_if'].reshape(B, S, 2, H)
    y_b = mlstm_mixer(qkv_b[:, :, 0], qkv_b[:, :, 1], qkv_b[:, :, 2],
                      if_b[:, :, 0] + mlstm_i_bias, if_b[:, :, 1] + mlstm_f_bias,
                      parts['mlstm_o'], mlstm_norm_g)
    y_b = y_b.astype(x.dtype) * jax.nn.silu(parts['mlstm_z'])

    y_c = rglru_mixer(parts['lru_x'], lru_conv_w, lru_conv_b, lru_gate_w, lru_gate_b, lru_lambda)
    y_c = y_c.astype(x.dtype) * jax.nn.silu(parts['lru_z'])

    qkv_d = parts['moba_qkv'].reshape(B, S, 3, H, dh)
    qd = partial_rotary(qkv_d[:, :, 0], pos)
    kd = partial_rotary(qkv_d[:, :, 1], pos)
    y_d = moba_mixer(qd, kd, qkv_d[:, :, 2])
    y_d = y_d.astype(x.dtype) * jax.nn.silu(parts['moba_z'])

    y = jnp.concatenate([y_a, y_b, y_c, y_d], axis=-1) @ w_out
    return layer_norm(DEEPNORM_ALPHA * x + y, ln_g, ln_b)


def setup_inputs(seed: int = 0) -> dict:
    key = jax.random.key(seed)
    ks = jax.random.split(key, 16)
    nrm = jax.random.normal
    x = nrm(ks[0], (BATCH, SEQ, D_MODEL), jnp.float32)
    w_in = nrm(ks[1], (DEPTH, D_MODEL, IN_WIDTH), jnp.float32) * D_MODEL ** -0.5
    nsa_cmp_w1 = nrm(ks[2], (DEPTH, 2, CMP_BLOCK * HEAD_DIM, CMP_HIDDEN), jnp.float32) * (CMP_BLOCK * HEAD_DIM) ** -0.5
    nsa_cmp_w2 = nrm(ks[3], (DEPTH, 2, CMP_HIDDEN, HEAD_DIM), jnp.float32) * CMP_HIDDEN ** -0.5
    nsa_cmp_pe = nrm(ks[4], (DEPTH, 2, CMP_BLOCK, HEAD_DIM), jnp.float32) * 0.1
    mlstm_i_bias = nrm(ks[5], (DEPTH, N_HEADS), jnp.float32) * 0.1
    mlstm_f_bias = jnp.linspace(3.0, 6.0, N_HEADS, dtype=jnp.float32)[None, :] + nrm(ks[6], (DEPTH, N_HEADS), jnp.float32) * 0.1
    mlstm_norm_g = 1.0 + nrm(ks[7], (DEPTH, GROUP_WIDTH), jnp.float32) * 0.02
    lru_conv_w = nrm(ks[8], (DEPTH, CONV_WIDTH, GROUP_WIDTH), jnp.float32) * CONV_WIDTH ** -0.5
    lru_conv_b = nrm(ks[9], (DEPTH, GROUP_WIDTH), jnp.float32) * 0.01
    lru_gate_w = nrm(ks[10], (DEPTH, 2, LRU_BLOCKS, LRU_BLOCK_WIDTH, LRU_BLOCK_WIDTH), jnp.float32) * LRU_BLOCK_WIDTH ** -0.5
    lru_gate_b = nrm(ks[11], (DEPTH, 2, GROUP_WIDTH), jnp.float32) * 0.01
    a_c = jax.random.uniform(ks[12], (DEPTH, GROUP_WIDTH), jnp.float32, minval=0.9, maxval=0.999)
    s_lam = a_c ** (1.0 / LRU_C)
    lru_lambda = jnp.log(s_lam) - jnp.log1p(-s_lam)
    w_out = nrm(ks[13], (DEPTH, MIX_WIDTH, D_MODEL), jnp.float32) * (MIX_WIDTH ** -0.5) * DEEPNORM_BETA
    ln_g = 1.0 + nrm(ks[14], (DEPTH, D_MODEL), jnp.float32) * 0.02
    ln_b = nrm(ks[15], (DEPTH, D_MODEL), jnp.float32) * 0.02
    return {'x': x, 'w_in': w_in, 'nsa_cmp_w1': nsa_cmp_w1, 'nsa_cmp_w2': nsa_cmp_w2,
            'nsa_cmp_pe': nsa_cmp_pe, 'mlstm_i_bias': mlstm_i_bias, 'mlstm_f_bias': mlstm_f_bias,
            'mlstm_norm_g': mlstm_norm_g, 'lru_conv_w': lru_conv_w, 'lru_conv_b': lru_conv_b,
            'lru_gate_w': lru_gate_w, 'lru_gate_b': lru_gate_b, 'lru_lambda': lru_lambda,
            'w_out': w_out, 'ln_g': ln_g, 'ln_b': ln_b}


def reference(x, w_in, nsa_cmp_w1, nsa_cmp_w2, nsa_cmp_pe, mlstm_i_bias, mlstm_f_bias, mlstm_norm_g,
              lru_conv_w, lru_conv_b, lru_gate_w, lru_gate_b, lru_lambda, w_out, ln_g, ln_b):
    pos = jnp.arange(x.shape[1], dtype=jnp.int32)
    for l in range(DEPTH):
        x = hybrid_layer(x, pos, w_in[l], nsa_cmp_w1[l], nsa_cmp_w2[l], nsa_cmp_pe[l],
                         mlstm_i_bias[l], mlstm_f_bias[l], mlstm_norm_g[l],
                         lru_conv_w[l], lru_conv_b[l], lru_gate_w[l], lru_gate_b[l], lru_lambda[l],
                         w_out[l], ln_g[l], ln_b[l])
    return x
```

```python
import contextlib
import numpy as np
import ml_dtypes
import concourse.bass as bass
import concourse.mybir as mybir
from concourse.alu_op_type import AluOpType as ALU
from concourse.bass_utils import run_bass_kernel_spmd

F32 = mybir.dt.float32
BF16 = mybir.dt.bfloat16
AF = mybir.ActivationFunctionType
AX = mybir.AxisListType
ENGS = ("pe", "act", "dve", "pool", "sp")
import os
FLAG_SPW = int(os.environ.get("K_SPW", "12"))
FLAG_CASTACT = int(os.environ.get("K_CASTACT", "0"))
FLAG_PREF = int(os.environ.get("K_PREF", "31"))
FLAG_DMACAST = int(os.environ.get("K_DMACAST", "1"))
FLAG_STRICT = int(os.environ.get("K_STRICT", "0"))

S = 2048
D = 2048
NT = 16
INW = 7444
NEGB = -30000.0
SCALE = 128 ** -0.5
ALPHA = float((2 * 2) ** 0.25)

C_NSA_Q = 0
C_NSA_KV = 512
C_NSA_G = 1280
C_NSA_Z = 1292
C_ML_QKV = 1804
C_ML_IF = 3340
C_ML_O = 3348
C_ML_Z = 3860
C_LRU_X = 4372
C_LRU_Z = 4884
C_MB_QKV = 5396
C_MB_Z = 6932


class Prog:
    def __init__(self, nc):
        self.nc = nc
        self.ops = []
        self.last_w = {}
        self.readers = {}
        self.slot_count = {}
        self.slot_order = []
        self.eng_h = {"pe": nc.tensor, "act": nc.scalar, "dve": nc.vector,
                      "pool": nc.gpsimd, "sp": nc.sync}

    def barrier(self):
        if not hasattr(self, "bar"):
            self.bar = {}
            self.since = {e: [] for e in ENGS}
        n1 = {}
        mk = getattr(self, "markers", {})
        for e in ENGS:
            if e == "pe":
                continue
            prev = list(self.since[e])
            fn = mk.get(e, (lambda h: h.nop()))
            n1[e] = self.op(e, fn, order_extra=prev, cost=0.1, _nosince=True)
        for e in ENGS:
            prev = list(self.since[e]) if e == "pe" else [n1[e]]
            self.bar[e] = self.op(e, lambda h: h.nop(), extra=[n1[f] for f in n1 if f != e],
                                  order_extra=prev, cost=0.05, _nosince=True)
            self.since[e] = []

    def op(self, eng, fn, reads=(), writes=(), slot=None, extra=(), cost=0.3, order_extra=(),
           _nosince=False):
        i = len(self.ops)
        raw = set(extra)
        oth = set(order_extra)
        if not hasattr(self, "since"):
            self.bar = {}
            self.since = {e: [] for e in ENGS}
        if not _nosince:
            self.since[eng].append(i)
        for r in reads:
            w = self.last_w.get(r)
            if w is not None:
                raw.add(w)
        for r in writes:
            w = self.last_w.get(r)
            if w is not None:
                oth.add(w)
            for rd in self.readers.get(r, ()):
                oth.add(rd)
        bi = getattr(self, "bar", {}).get(eng)
        if bi is not None:
            oth.add(bi)
        raw.discard(i)
        oth.discard(i)
        o = dict(eng=eng, fn=fn, raw=raw, oth=oth - raw, slot=slot, i=i, cost=cost)
        if slot is not None:
            if slot not in self.slot_count:
                self.slot_count[slot] = 0
                self.slot_order.append(slot)
            self.slot_count[slot] += 1
            o["slot_n"] = self.slot_count[slot]
        self.ops.append(o)
        for r in reads:
            self.readers.setdefault(r, []).append(i)
        for r in writes:
            self.last_w[r] = i
            self.readers[r] = []
        return i

    def schedule(self, window=int(os.environ.get('K_WIN', '100')), lat=0.25):
        ops = self.ops
        n = len(ops)
        succ = [[] for _ in range(n)]
        for o in ops:
            order = o["raw"] | o["oth"]
            sem = set()
            for d in order:
                p = ops[d]
                if p["slot"] is not None or o["slot"] is not None:
                    sem.add(d)
                elif p["eng"] != o["eng"]:
                    sem.add(d)
                elif d in o["raw"] and o["eng"] != "pe":
                    sem.add(d)
                elif FLAG_STRICT and p["fn"] is not None and not p.get("isnop"):
                    sem.add(d)
            o["order"], o["deps"] = order, sem
            o["nun"] = len(order)
            o["ready"] = 0.0
            for d in order:
                succ[d].append(o["i"])
        pend = {e: [o["i"] for o in ops if o["eng"] == e] for e in ENGS}
        head = {e: 0 for e in ENGS}
        done = [False] * n
        fin = [0.0] * n
        clock = {e: 0.0 for e in ENGS}
        streams = {e: [] for e in ENGS}
        left = n
        while left:
            best = None
            for e in ENGS:
                lst = pend[e]
                h = head[e]
                while h < len(lst) and done[lst[h]]:
                    h += 1
                head[e] = h
                cnt = 0
                j = h
                w = FLAG_SPW if e == "sp" else window
                while j < len(lst) and cnt < w:
                    idx = lst[j]
                    j += 1
                    if done[idx]:
                        continue
                    cnt += 1
                    o = ops[idx]
                    if o["nun"] == 0:
                        st = max(clock[e], o["ready"])
                        key = (st, idx)
                        if best is None or key < best[0]:
                            best = (key, e, idx)
            assert best is not None, "scheduler deadlock"
            (st, idx), e, _ = best
            o = ops[idx]
            done[idx] = True
            left -= 1
            if o["slot"] is not None:
                clock[e] = st + 0.06
            else:
                clock[e] = st + o["cost"]
            fin[idx] = st + o["cost"]
            streams[e].append(idx)
            for s_ in succ[idx]:
                so = ops[s_]
                so["nun"] -= 1
                r = fin[idx] + (lat if (so["eng"] != e or o["slot"] is not None) else 0.0)
                if r > so["ready"]:
                    so["ready"] = r
        self.sim_time = max(fin) if n else 0.0
        return streams

    def emit(self, final_wait_eng="sp"):
        nc = self.nc
        ops = self.ops
        streams = self.schedule()
        pos = {}
        for e in ENGS:
            for k_, idx in enumerate(streams[e]):
                pos[idx] = k_
        for o in ops:
            best = {}
            for d in o["deps"]:
                p = ops[d]
                if p["slot"] is not None:
                    key = ("s", p["slot"])
                    val = p["slot_n"]
                else:
                    key = ("t", p["eng"])
                    val = pos[d]
                if key not in best or val > best[key][0]:
                    best[key] = (val, d)
            o["deps"] = {d for (_, d) in best.values()}
        has_dep = [False] * len(ops)
        for o in ops:
            for d in o["deps"]:
                has_dep[d] = True
        tick = {e: 0 for e in ENGS}
        for e in ENGS:
            for idx in streams[e]:
                o = ops[idx]
                if o["slot"] is None and has_dep[idx]:
                    tick[e] += 1
                    o["tick"] = tick[e]
        with contextlib.ExitStack() as st:
            tsem = {e: st.enter_context(nc.semaphore("tk_" + e)) for e in ENGS}
            ssem = {s: st.enter_context(nc.semaphore("sl_%d" % k))
                    for k, s in enumerate(self.slot_order)}
            block = st.enter_context(nc.Block())
            total_slots = dict(self.slot_count)

            self.trace_ev = {e: [] for e in ENGS}

            def stream(eng):
                h = self.eng_h[eng]
                waited = {}
                ev = self.trace_ev[eng]
                for idx in streams[eng]:
                    o = ops[idx]
                    need = {}
                    for d in o["deps"]:
                        p = ops[d]
                        if p["slot"] is not None:
                            key = ("s", p["slot"])
                            val = 16 * p["slot_n"]
                            sem = ssem[p["slot"]]
                        else:
                            key = ("t", p["eng"])
                            val = p["tick"]
                            sem = tsem[p["eng"]]
                        if val > need.get(key, (None, 0))[1]:
                            need[key] = (sem, val)
                    for key, (sem, val) in need.items():
                        if waited.get(key, 0) >= val:
                            continue
                        h.wait_ge(sem, val)
                        waited[key] = val
                        ev.append(("w", key, val, idx))
                    ins = o["fn"](h)
                    if o["slot"] is not None:
                        ins.then_inc(ssem[o["slot"]], 16)
                        ev.append(("i", ("s", o["slot"]), 16, idx))
                    elif has_dep[idx]:
                        ins.then_inc(tsem[eng], 1)
                        ev.append(("i", ("t", eng), 1, idx))
                if eng == final_wait_eng:
                    for s, n_ in total_slots.items():
                        h.wait_ge(ssem[s], 16 * n_)
                    for e in ENGS:
                        if tick[e] > 0:
                            h.wait_ge(tsem[e], tick[e])

            @block.tensor
            def _(e):
                stream("pe")

            @block.scalar
            def _(e):
                stream("act")

            @block.vector
            def _(e):
                stream("dve")

            @block.gpsimd
            def _(e):
                stream("pool")

            @block.sync
            def _(e):
                stream("sp")
        return tick


class K:
    def __init__(self, nc):
        self.nc = nc
        self.P = Prog(nc)
        self.st = contextlib.ExitStack()
        self.ps_rr = 0

    def sb(self, name, shape, dt):
        return self.st.enter_context(self.nc.sbuf_tensor(name, list(shape), dt)).ap()

    def ps(self, name, shape, dt):
        return self.st.enter_context(self.nc.psum_tensor(name, list(shape), dt)).ap()

    @staticmethod
    def nf(ap):
        return int(np.prod(ap.shape[1:]))

    def mm(self, out, lhsT, rhs, start, stop, reads, writes):
        n = self.nf(rhs)
        c = max(290, n) / 2400.0 * (4.0 if rhs.dtype == F32 else 1.0) + 0.01
        self.P.op("pe", lambda e: e.matmul(out, lhsT=lhsT, rhs=rhs, start=start, stop=stop),
                  reads=reads, writes=writes, cost=c)

    def tr(self, out, in_, ident, reads, writes):
        self.P.op("pe", lambda e: e.transpose(out, in_, ident), reads=reads, writes=writes, cost=0.07)

    def act(self, out, in_, func, reads, writes, bias=None, scale=None, eng="act"):
        kw = {}
        if bias is not None:
            kw["bias"] = bias
        if scale is not None:
            kw["scale"] = scale
        self.P.op(eng, lambda e: e.activation(out=out, in_=in_, func=func, **kw),
                  reads=reads, writes=writes, cost=(224 + self.nf(out)) / 1200.0)

    def ecost(self, eng, out):
        n = self.nf(out)
        if eng == "act":
            return (224 + n) / 1200.0
        if eng == "pool":
            return 0.25 + n * 2.5e-3
        return 0.12 + n / 960.0

    def cp(self, eng, out, in_, reads, writes):
        c = self.ecost(eng, out)
        if eng == "act":
            self.P.op("act", lambda e: e.activation(out=out, in_=in_, func=AF.Copy),
                      reads=reads, writes=writes, cost=c)
        else:
            self.P.op(eng, lambda e: e.tensor_copy(out=out, in_=in_), reads=reads, writes=writes, cost=c)

    def tt(self, eng, out, in0, in1, op, reads, writes):
        self.P.op(eng, lambda e: e.tensor_tensor(out=out, in0=in0, in1=in1, op=op),
                  reads=reads, writes=writes, cost=self.ecost(eng, out))

    def ts(self, eng, out, in0, s1, op0, reads, writes, s2=None, op1=None):
        c = self.ecost(eng, out)
        if op1 is None:
            self.P.op(eng, lambda e: e.tensor_scalar(out=out, in0=in0, scalar1=s1, scalar2=None, op0=op0),
                      reads=reads, writes=writes, cost=c)
        else:
            self.P.op(eng, lambda e: e.tensor_scalar(out=out, in0=in0, scalar1=s1, scalar2=s2,
                                                     op0=op0, op1=op1),
                      reads=reads, writes=writes, cost=c)

    def stt(self, out, in0, scalar, in1, op0, op1, reads, writes):
        self.P.op("dve", lambda e: e.scalar_tensor_tensor(out=out, in0=in0, scalar=scalar, in1=in1,
                                                          op0=op0, op1=op1),
                  reads=reads, writes=writes, cost=self.ecost("dve", out))

    def dma(self, out, in_, reads, writes, slot, eng="sp", nc_ok=False):
        nbytes = self.nf(out) * out.shape[0] * (4 if out.dtype == F32 else 2)
        c = 2.0 + nbytes / 150e3
        if nc_ok:
            self.P.op(eng, lambda e: e.dma_start(out=out, in_=in_, allow_slow_non_contiguous=True),
                      reads=reads, writes=writes, slot=slot, cost=c)
        else:
            self.P.op(eng, lambda e: e.dma_start(out=out, in_=in_), reads=reads, writes=writes, slot=slot, cost=c)

    def memset(self, eng, ap, val, writes):
        self.P.op(eng, lambda e: e.memset(ap, val), reads=(), writes=writes, cost=self.ecost(eng, ap))


ARENA_BYTES = 106 * 1024


class KB(K):
    def __init__(self, nc, dbg=False):
        super().__init__(nc)
        self.dbg = dbg
        self.aoff = 0
        self.cnt = 0

    def setup(self):
        self.xT = self.sb("xT", [128, 16, S], BF16)
        self.wb = [self.sb("wb%d" % i, [128, 16, 512], BF16) for i in range(2)]
        self.identf = self.sb("identf", [128, 128], F32)
        self.identb = self.sb("identb", [128, 128], BF16)
        self.arena = self.sb("arena", [128, ARENA_BYTES // 2], BF16)
        bscr = self.sb("bscr", [128, 4], F32)
        self.P.markers = {
            "act": lambda h: h.activation(out=bscr[:, 0:1], in_=bscr[:, 3:4], func=AF.Copy),
            "dve": lambda h: h.tensor_copy(out=bscr[:, 1:2], in_=bscr[:, 3:4]),
            "pool": lambda h: h.tensor_copy(out=bscr[:, 2:3], in_=bscr[:, 3:4]),
        }
        self.pst = [self.ps("psb%d" % i, [128, 512], F32) for i in range(8)]
        self.ws_rr = 0
        self.wb_rr = 0

    def psn(self, k):
        return "ps%d" % k

    def carve(self, shape, dt):
        esz = 4 if dt == F32 else 2
        n = int(np.prod(shape[1:]))
        nbytes = (n * esz + 63) // 64 * 64
        off = self.aoff
        self.aoff += nbytes
        assert self.aoff <= ARENA_BYTES, ("arena overflow", self.aoff)
        v = self.arena[:, off // 2:(off + n * esz) // 2]
        if dt == F32:
            v = v.bitcast(F32)
        if len(shape) == 3:
            v = v.rearrange("p (a b) -> p a b", a=shape[1])
        elif len(shape) == 4:
            v = v.rearrange("p (a b c) -> p a b c", a=shape[1], b=shape[2])
        if shape[0] < 128:
            v = v[0:shape[0]]
        return v

    def phase_end(self):
        self.P.barrier()
        self.aoff = 0

    def run_pref(self):
        q = getattr(self, "pref_q", None)
        if q:
            self.pref = q.pop(0)()

    def take_pref(self, fn):
        p = getattr(self, "pref", None)
        self.pref = None
        return p if p is not None else fn()

    def uid(self, s):
        self.cnt += 1
        return "%s_%d" % (s, self.cnt)

    def const_in(self, name, arr):
        dt = BF16 if arr.dtype == ml_dtypes.bfloat16 else F32
        t = self.nc.dram_tensor(name, list(arr.shape), dt, kind="ExternalInput").ap()
        self.consts[name] = arr
        return t

    def load_const(self, name, dram_ap, shape, dt):
        t = self.carve(shape, dt)
        self.dma(t, dram_ap, reads=[], writes=[name], slot="c_" + name)
        return t

    def wload(self, wsrc, ncols, dst=None, dcol0=0, dres=None):
        if dst is None:
            slot = self.wb_rr
            self.wb_rr ^= 1
            dst = self.wb[slot]
            dres = ("wb", slot)
        else:
            slot = None
        if FLAG_DMACAST:
            dl = dres if isinstance(dres, list) else [dres]
            tag = "xT" if slot is None else "wb%d" % slot
            for g in range(4):
                src = wsrc[g * 512:(g + 1) * 512, :].rearrange("(c p) n -> p c n", p=128)
                self.dma(dst[:, g * 4:(g + 1) * 4, dcol0:dcol0 + ncols], src, reads=[], writes=dl,
                         slot=("wcast", tag, g), eng="pool")
            return slot
        for g in range(4):
            s = self.ws_rr
            self.ws_rr ^= 1
            stg = self.wstg[s]
            src = wsrc[g * 512:(g + 1) * 512, :].rearrange("(c p) n -> p c n", p=128)
            self.dma(stg[:, :, 0:ncols], src, reads=[], writes=[("wstg", s)], slot=("wstg", s))
            self.cp("act" if (g % 2 == 0 and FLAG_CASTACT) else "pool", dst[:, g * 4:(g + 1) * 4, dcol0:dcol0 + ncols], stg[:, :, 0:ncols],
                    reads=[("wstg", s)], writes=[dres] if not isinstance(dres, list) else dres)
        return slot

    def allxt(self):
        return [("xT", t) for t in range(NT)]

    def proj_fm(self, slot, c0, g, k):
        for dc in range(16):
            self.mm(self.pst[k][:, :], self.wb[slot][:, dc, c0:c0 + 128],
                    self.xT[:, dc, g * 512:(g + 1) * 512], dc == 0, dc == 15,
                    reads=[("wb", slot)] + [("xT", t) for t in range(4 * g, 4 * g + 4)],
                    writes=[self.psn(k)])

    def proj_tm(self, slot, c0, ncols, t, k):
        for dc in range(16):
            self.mm(self.pst[k][:, 0:ncols], self.xT[:, dc, t * 128:(t + 1) * 128],
                    self.wb[slot][:, dc, c0:c0 + ncols], dc == 0, dc == 15,
                    reads=[("wb", slot), ("xT", t)], writes=[self.psn(k)])

    def phase0(self, xsrc):
        if getattr(self, "pref", None) is None:
            self.run_pref()
        save = self.aoff
        self.aoff = ARENA_BYTES - 2 * 8192 - 2 * 4096
        xin = [self.carve([128, 2048], F32) for _ in range(2)]
        xbf = [self.carve([128, 2048], BF16) for _ in range(2)]
        self.aoff = save
        for t in range(NT):
            b = xin[t % 2]
            bb = xbf[t % 2]
            self.dma(b, xsrc[t * 128:(t + 1) * 128, :], reads=[], writes=[("xin", t % 2)],
                     slot=("xin", t % 2))
            self.cp("act" if t % 2 == 0 else "dve", bb, b, reads=[("xin", t % 2)], writes=[("xbf", t % 2)])
            for g in range(2):
                k = (t * 2 + g) % 8
                pb = self.pst[k][:, :].bitcast(BF16)
                for j in range(8):
                    dc = g * 8 + j
                    self.tr(pb[:, j * 128:(j + 1) * 128], bb[:, dc * 128:(dc + 1) * 128],
                            self.identb, reads=[("xbf", t % 2), "identb"], writes=[self.psn(k)])
                self.cp("dve" if t % 2 == 0 else "act",
                        self.xT[:, g * 8:(g + 1) * 8, t * 128:(t + 1) * 128],
                        pb.rearrange("p (c q) -> p c q", c=8),
                        reads=[self.psn(k)], writes=[("xT", t)])
        self.p0_top = ARENA_BYTES - 2 * 8192 - 2 * 4096

    def rope_setup(self):
        self.cosT = self.load_const("cosT", self.cst["cosT"], [32, S], F32)
        self.sinT = self.load_const("sinT", self.cst["sinT"], [32, S], F32)
        self.rotP = self.load_const("rotP", self.cst["rotP"], [32, 32], F32)
        self.qf = [self.carve([128, 512], F32) for _ in range(2)]
        self.rt = [self.carve([32, 512], F32) for _ in range(2)]
        self.rope_i = 0

    def rope_evac(self, k, dst, g, dres, krot=7):
        i = self.rope_i % 2
        self.rope_i += 1
        qf = self.qf[i]
        t1, t2 = self.rt
        rq, r1, r2 = ("qf", i), "rt1", "rt2"
        sl = slice(g * 512, (g + 1) * 512)
        self.cp("act", qf[0:32, :], self.pst[k][0:32, :], reads=[self.psn(k)], writes=[rq])
        self.mm(self.pst[krot][0:32, :], self.rotP, qf[0:32, :], True, True,
                reads=[rq, "rotP"], writes=[self.psn(krot)])
        self.tt("dve", t1, qf[0:32, :], self.cosT[:, sl], ALU.mult, reads=[rq, "cosT"], writes=[r1])
        self.tt("dve", t2, self.pst[krot][0:32, :], self.sinT[:, sl], ALU.mult,
                reads=[self.psn(krot), "sinT"], writes=[r2])
        self.tt("dve", dst[0:32, :], t1, t2, ALU.add, reads=[r1, r2], writes=[dres])
        self.cp("act", dst[32:64, :], self.pst[k][32:64, :], reads=[self.psn(k)], writes=[dres])
        self.cp("act", dst[64:128, :], self.pst[k][64:128, :], reads=[self.psn(k)], writes=[dres])

    def group_nsa(self, L):
        w = self.w_in[L]
        cst = self.cst
        qT = self.carve([128, 4, S], BF16)
        kT = self.carve([128, 3, S], BF16)
        vcT = self.carve([128, S], BF16)
        v1 = self.carve([128, 2, NT, 130], BF16)
        gsig = self.carve([128, NT, 12], F32)
        kcT = self.carve([128, 128], BF16)
        vc = self.carve([128, 128], BF16)
        w1b = self.carve([128, 2, 32, 128], BF16)
        w1 = self.w_cmp1[L]
        for m in range(2):
            for q4 in range(4):
                src = w1[m, q4 * 1024:(q4 + 1) * 1024, :].rearrange("(a p) j -> p a j", p=128)
                self.dma(w1b[:, m, q4 * 8:(q4 + 1) * 8, :], src, reads=[], writes=["w1b"],
                         slot=("w1cast", m, q4), eng="pool")
        mark = self.aoff
        self.rope_setup()
        self.memset("pool", v1[:, :, :, 128:129], 1.0, writes=["v1"])
        slot = self.take_pref(lambda: self.wload(w[:, C_NSA_Q:C_NSA_Q + 512], 512))
        kk = 0
        for g in range(4):
            for h in range(4):
                k = kk % 6
                kk += 1
                self.proj_fm(slot, h * 128, g, k)
                self.rope_evac(k, qT[:, h, g * 512:(g + 1) * 512], g, "qT")
        slot = self.wload(w[:, C_NSA_KV:C_NSA_KV + 512], 512)
        for br in range(3):
            for g in range(4):
                k = kk % 6
                kk += 1
                self.proj_fm(slot, br * 128, g, k)
                self.rope_evac(k, kT[:, br, g * 512:(g + 1) * 512], g, "kT")
        for g in range(4):
            k = kk % 6
            kk += 1
            self.proj_fm(slot, 384, g, k)
            self.cp("act", vcT[:, g * 512:(g + 1) * 512], self.pst[k][:, :],
                    reads=[self.psn(k)], writes=["vcT"])
        slot = self.wload(w[:, C_NSA_KV + 512:C_NSA_KV + 512 + 268], 268)
        for t in range(NT):
            k = kk % 6
            kk += 1
            self.proj_tm(slot, 0, 268, t, k)
            self.cp("dve", v1[:, :, t, 0:128],
                    self.pst[k][:, 0:256].rearrange("p (b d) -> p b d", b=2),
                    reads=[self.psn(k)], writes=["v1"])
            self.act(gsig[:, t, :], self.pst[k][:, 256:268], AF.Sigmoid,
                     reads=[self.psn(k)], writes=["gsig"])
        zslot = self.wload(w[:, C_NSA_Z:C_NSA_Z + 512], 512)
        assert self.aoff <= self.p0_top, (self.aoff, self.p0_top)
        self.P.barrier()
        self.aoff = mark
        self.run_pref()

        w1 = self.w_cmp1[L]
        w2 = self.w_cmp2[L]
        pe = self.w_pe[L]
        w2b = self.carve([128, 2, 128], BF16)
        w2f = self.carve([128, 2, 128], F32)
        pef = self.carve([32, 2, 128], F32)
        peT = self.carve([128, 2, 32], BF16)
        cvec = self.carve([128, 2], F32)
        hid = self.carve([128, 2, 128], BF16)
        self.dma(w2f, w2.rearrange("m j d -> j m d"), reads=[], writes=["w2f"], slot="w2f")
        self.cp("pool", w2b, w2f, reads=["w2f"], writes=["w2b"])
        self.dma(pef, pe.rearrange("m p d -> p m d"), reads=[], writes=["pef"], slot="pef")
        for m in range(2):
            self.tr(self.pst[6][:, m * 32:(m + 1) * 32], pef[:, m, :], self.identf[0:32, 0:32],
                    reads=["pef", "identf"], writes=[self.psn(6)])
        self.cp("dve", peT, self.pst[6][:, 0:64].rearrange("p (m a) -> p m a", m=2),
                reads=[self.psn(6)], writes=["peT"])
        self.memset("pool", kcT, 0.0, writes=["kcT"])
        self.memset("pool", vc, 0.0, writes=["vc"])
        self.memset("pool", hid, 0.0, writes=["hid"])
        srcT = [kT[:, 0, :], vcT]
        for m in range(2):
            for p in range(32):
                self.mm(self.pst[6][:, 64 + m:65 + m], w1b[:, m, p, :], peT[:, m, p:p + 1], p == 0, p == 31,
                        reads=["w1b", "peT"], writes=[self.psn(6)])
            self.cp("dve", cvec[:, m:m + 1], self.pst[6][:, 64 + m:65 + m], reads=[self.psn(6)], writes=["cvec"])
            for p in range(32):
                self.mm(self.pst[m][:, 0:127], w1b[:, m, p, :], srcT[m][:, p:p + 16 * 126 + 1:16],
                        p == 0, p == 31, reads=["w1b", "kT", "vcT"], writes=[self.psn(m)])
            self.act(hid[:, m, 0:127], self.pst[m][:, 0:127], AF.Silu, bias=cvec[:, m:m + 1],
                     reads=[self.psn(m), "cvec"], writes=["hid"])
        self.mm(self.pst[2][:, 0:127], w2b[:, 0, :], hid[:, 0, 0:127], True, True,
                reads=["w2b", "hid"], writes=[self.psn(2)])
        self.cp("dve", kcT[:, 0:127], self.pst[2][:, 0:127], reads=[self.psn(2)], writes=["kcT"])
        self.mm(self.pst[3][0:127, 0:128], hid[:, 1, 0:127], w2b[:, 1, :], True, True,
                reads=["w2b", "hid"], writes=[self.psn(3)])
        self.cp("dve", vc[0:127, :], self.pst[3][0:127, 0:128], reads=[self.psn(3)], writes=["vc"])

        cmpb = self.load_const("cmpb", cst["cmpb"], [128, S], BF16)
        esel = self.load_const("esel", cst["esel"], [32, NT, 128], BF16)
        cbd = self.load_const("cbd", cst["cbd"], [128, 128], BF16)
        cbw = self.load_const("cbw", cst["cbw"], [128, 128], BF16)
        ovl = self.load_const("ovl", cst["ovl"], [128, 34], BF16)
        m1 = self.load_const("m1", cst["m1"], [128, NT, 32], F32)
        m2 = self.load_const("m2", cst["m2"], [128, NT, 32], F32)
        selT = [self.carve([32, 128], BF16) for _ in range(2)]
        eb = [self.carve([128, 512], BF16) for _ in range(3)]
        U = self.carve([128, 4, 34], F32)
        t32 = self.carve([128, 4, 32], F32)
        imp = self.carve([128, 32], F32)
        top8 = self.carve([128, 8], F32)
        selm = self.carve([128, 32], BF16)
        Lall = self.carve([128, 12], F32)
        coef = self.carve([128, 12], F32)
        acc = [self.carve([128, 512], F32) for _ in range(2)]
        sz = [self.carve([128, 512], F32) for _ in range(2)]
        yo = [self.carve([128, 512], BF16) for _ in range(2)]
        ei = 0
        ocn = [self.carve([128, 512], F32) for _ in range(2)]
        rLc = self.carve([128, 4], F32)
        self.memset("pool", Lall[:, 0:4], 1.0, writes=["Lall"])
        eic = [0]

        def cmp_stage(t):
            qs = slice(t * 128, (t + 1) * 128)
            qv = qT[:, :, qs]
            sT = selT[t % 2]
            sTr = ("selT", t % 2)
            kS = eic[0] % 2
            e = eb[eic[0] % 3]
            er = ("eb", eic[0] % 3)
            eic[0] += 1
            sv = self.pst[kS][:, :].rearrange("p (h q) -> p h q", h=4)
            self.mm(sv, kcT, qv, True, False, reads=["kcT", "qT"], writes=[self.psn(kS)])
            self.mm(sv, self.identb, cmpb[:, qs].unsqueeze(1).broadcast_to([128, 4, 128]), False, True,
                    reads=["cmpb", "identb"], writes=[self.psn(kS)])
            self.act(e, self.pst[kS][:, :], AF.Exp, scale=SCALE, reads=[self.psn(kS)], writes=[er])
            for h in range(4):
                self.mm(self.pst[6][:, h * 128:(h + 1) * 128], e[:, h * 128:(h + 1) * 128], vc, h == 0, h == 3,
                        reads=[er, "vc"], writes=[self.psn(6)])
            for h in range(4):
                self.mm(self.pst[7][:, h * 34:(h + 1) * 34], e[:, h * 128:(h + 1) * 128], ovl, h == 0, h == 3,
                        reads=[er, "ovl"], writes=[self.psn(7)])
            self.cp("dve", U, self.pst[7][:, 0:136].rearrange("p (h j) -> p h j", h=4),
                    reads=[self.psn(7)], writes=["U"])
            self.ts("dve", rLc, U[:, :, 32], 1e-30, ALU.max, reads=["U"], writes=["rLc"])
            self.P.op("dve", lambda e_, o=rLc, i=rLc: e_.reciprocal(out=o, in_=i),
                      reads=["rLc"], writes=["rLc"])
            self.tt("dve", ocn[t % 2].rearrange("p (h d) -> p h d", h=4),
                    self.pst[6][:, :].rearrange("p (h d) -> p h d", h=4),
                    rLc.unsqueeze(2).broadcast_to([128, 4, 128]), ALU.mult,
                    reads=[self.psn(6), "rLc"], writes=[("ocn", t % 2)])
            self.tt("dve", t32, U[:, :, 0:32], rLc.unsqueeze(2).broadcast_to([128, 4, 32]), ALU.mult,
                    reads=["U", "rLc"], writes=["t32"])
            self.P.op("dve", lambda e_, o=imp, i=t32.rearrange("p h j -> p j h"): e_.tensor_reduce(
                out=o, in_=i, axis=AX.X, op=ALU.add), reads=["t32"], writes=["imp"])
            self.tt("dve", imp, imp, m1[:, t, :], ALU.mult, reads=["imp", "m1"], writes=["imp"])
            self.tt("dve", imp, imp, m2[:, t, :], ALU.add, reads=["imp", "m2"], writes=["imp"])
            self.P.op("dve", lambda e_, o=top8, i=imp: e_.max(out=o, in_=i), reads=["imp"], writes=["top8"])
            self.ts("dve", selm, imp, top8[:, 7:8], ALU.is_ge, reads=["imp", "top8"], writes=["selm"],
                    s2=-1.0, op1=ALU.add)
            pb = self.pst[7][:, :].bitcast(BF16)
            self.tr(pb[0:32, 0:128], selm, self.identb, reads=["selm", "identb"], writes=[self.psn(7)])
            self.cp("act", sT, pb[0:32, 0:128], reads=[self.psn(7)], writes=[sTr])

        cmp_stage(0)
        for t in range(NT):
            qs = slice(t * 128, (t + 1) * 128)
            qv = qT[:, :, qs]
            sT = selT[t % 2]
            sTr = ("selT", t % 2)
            if t + 1 < NT:
                cmp_stage(t + 1)
            self.proj_tm(zslot, 0, 512, t, 7)
            self.act(sz[t % 2], self.pst[7][:, :], AF.Silu, reads=[self.psn(7)], writes=[("sz", t % 2)])
            for br, kts in ((2, [kt for kt in (t - 2, t - 1, t) if kt >= 0]), (1, list(range(0, t + 1)))):
                ob = (2, 3) if br == 1 else (4, 5)
                for ii, kt in enumerate(kts):
                    kS = eic[0] % 2
                    e = eb[eic[0] % 3]
                    er = ("eb", eic[0] % 3)
                    eic[0] += 1
                    ks = slice(kt * 128, (kt + 1) * 128)
                    sv = self.pst[kS][:, :].rearrange("p (h q) -> p h q", h=4)
                    extra = []
                    if br == 1:
                        extra.append((esel[:, kt, :], sT.unsqueeze(1).broadcast_to([32, 4, 128]),
                                      ["esel", sTr]))
                    if kt == t:
                        extra.append((self.identb, cbd.unsqueeze(1).broadcast_to([128, 4, 128]),
                                      ["cbd", "identb"]))
                    elif br == 2 and kt == t - 2:
                        extra.append((self.identb, cbw.unsqueeze(1).broadcast_to([128, 4, 128]),
                                      ["cbw", "identb"]))
                    self.mm(sv, kT[:, br, ks], qv, True, len(extra) == 0, reads=["kT", "qT"], writes=[self.psn(kS)])
                    for xi, (l_, r_, rd) in enumerate(extra):
                        self.mm(sv, l_, r_, False, xi == len(extra) - 1, reads=rd, writes=[self.psn(kS)])
                    self.act(e, self.pst[kS][:, :], AF.Exp, scale=SCALE, reads=[self.psn(kS)], writes=[er])
                    for h in range(4):
                        kb = ob[h // 2]
                        c0 = (h % 2) * 130
                        self.mm(self.pst[kb][:, c0:c0 + 129], e[:, h * 128:(h + 1) * 128], v1[:, br - 1, kt, 0:129],
                                (ii == 0 and h % 2 == 0), (ii == len(kts) - 1),
                                reads=[er, "v1"], writes=[self.psn(kb)])
            for bi, ob in enumerate(((2, 3), (4, 5))):
                for hh in range(2):
                    self.cp("dve", Lall[:, 4 + bi * 4 + hh * 2:4 + bi * 4 + hh * 2 + 2],
                            self.pst[ob[hh]][:, 0:260].rearrange("p (h c) -> p h c", h=2)[:, :, 128],
                            reads=[self.psn(ob[hh])], writes=["Lall"])
            self.ts("dve", Lall[:, 4:12], Lall[:, 4:12], 1e-30, ALU.max, reads=["Lall"], writes=["Lall"])
            self.P.op("dve", lambda e_, o=Lall[:, 4:12], i=Lall[:, 4:12]: e_.reciprocal(out=o, in_=i),
                      reads=["Lall"], writes=["Lall"])
            gv = gsig[:, t, :].rearrange("p (h b) -> p b h", b=3)
            self.tt("dve", coef.rearrange("p (b h) -> p b h", b=3), gv, Lall.rearrange("p (b h) -> p b h", b=3),
                    ALU.mult, reads=["gsig", "Lall"], writes=["coef"])
            a_ = acc[t % 2]
            ar = ("acc", t % 2)
            for h in range(4):
                hs = slice(h * 128, (h + 1) * 128)
                c0 = (h % 2) * 130
                self.ts("dve", a_[:, hs], ocn[t % 2][:, hs], coef[:, h:h + 1], ALU.mult,
                        reads=[("ocn", t % 2), "coef"], writes=[ar])
                for bi, ob in enumerate(((2, 3), (4, 5))):
                    self.stt(a_[:, hs], self.pst[ob[h // 2]][:, c0:c0 + 128],
                             coef[:, 4 * (bi + 1) + h:4 * (bi + 1) + h + 1],
                             a_[:, hs], ALU.mult, ALU.add, reads=[self.psn(ob[h // 2]), "coef", ar], writes=[ar])
            if self.dbg:
                self.dma(self.dbg_ya[qs, :], a_, reads=[ar], writes=[], slot="dbgya")
            self.tt("pool", yo[t % 2], a_, sz[t % 2], ALU.mult, reads=[ar, ("sz", t % 2)], writes=[("yo", t % 2)])
            self.dma(self.y_tm[qs, 0:512], yo[t % 2], reads=[("yo", t % 2)], writes=["y_tm"], slot=("yo", t % 2))
        self.phase_end()


    def group_moba(self, L):
        w = self.w_in[L]
        cst = self.cst
        qT = self.carve([128, 4, S], BF16)
        kT = self.carve([128, 4, S], BF16)
        v1 = self.carve([128, 4, NT, 130], BF16)
        szall = self.carve([128, NT, 512], BF16)
        mark = self.aoff
        self.rope_setup()
        self.memset("pool", v1[:, :, :, 128:129], 1.0, writes=["v1"])
        kk = 0
        for dstT, c0 in ((qT, C_MB_QKV), (kT, C_MB_QKV + 512)):
            if dstT is qT:
                slot = self.take_pref(lambda: self.wload(w[:, c0:c0 + 512], 512))
            else:
                slot = self.wload(w[:, c0:c0 + 512], 512)
            for h in range(4):
                for g in range(4):
                    k = kk % 6
                    kk += 1
                    self.proj_fm(slot, h * 128, g, k)
                    self.rope_evac(k, dstT[:, h, g * 512:(g + 1) * 512], g, "qT" if dstT is qT else "kT")
        slot = self.wload(w[:, C_MB_QKV + 1024:C_MB_QKV + 1536], 512)
        for t in range(NT):
            k = kk % 6
            kk += 1
            self.proj_tm(slot, 0, 512, t, k)
            self.cp("dve" if t % 2 else "act", v1[:, :, t, 0:128],
                    self.pst[k][:, :].rearrange("p (h d) -> p h d", h=4), reads=[self.psn(k)], writes=["v1"])
        zslot = self.wload(w[:, C_MB_Z:C_MB_Z + 512], 512)
        for t in range(NT):
            k = kk % 6
            kk += 1
            self.proj_tm(zslot, 0, 512, t, k)
            self.act(szall[:, t, :], self.pst[k][:, :], AF.Silu, reads=[self.psn(k)], writes=["szall"])
        self.run_pref()
        kmf = self.carve([128, 32], F32)
        kmb = self.carve([128, 4, 8], BF16)
        self.P.op("dve", lambda e_, o=kmf, i=kT.rearrange("p h (b k) -> p (h b) k", b=8): e_.tensor_reduce(
            out=o, in_=i, axis=AX.X, op=ALU.add), reads=["kT"], writes=["kmf"], cost=8.6)
        self.ts("dve", kmb.rearrange("p h b -> p (h b)"), kmf, 1.0 / 256.0, ALU.mult, reads=["kmf"], writes=["kmb"])
        e8 = self.load_const("e8", cst["e8"], [8, NT, 128], BF16)
        cbd = self.load_const("cbd", cst["cbd"], [128, 128], BF16)
        pastb = self.load_const("pastb", cst["pastb"], [128, NT, 8], F32)
        pastm = self.load_const("pastm", cst["pastm"], [128, NT, 8], F32)
        gs2 = self.carve([128, 4, 8], F32)
        top8 = self.carve([128, 4, 8], F32)
        selm = self.carve([128, 4, 8], F32)
        selmb = self.carve([128, 4, 8], BF16)
        sTs = [self.carve([8, 4, 128], BF16) for _ in range(2)]
        eb = [self.carve([128, 512], BF16) for _ in range(3)]
        Lall = self.carve([128, 4], F32)
        acc = [self.carve([128, 512], F32) for _ in range(2)]
        yo = [self.carve([128, 512], BF16) for _ in range(2)]
        ei = 0
        for t in range(NT):
            qs = slice(t * 128, (t + 1) * 128)
            cur = t // 2
            sT = sTs[t % 2]
            sTr = ("sT", t % 2)
            ob = (2, 3) if t % 2 == 0 else (4, 5)
            if cur > 0:
                for h in range(4):
                    self.mm(self.pst[7][:, h * 8:(h + 1) * 8], qT[:, h, qs], kmb[:, h, :], h == 0, h == 3,
                            reads=["qT", "kmb"], writes=[self.psn(7)])
                self.tt("dve", gs2, self.pst[7][:, 0:32].rearrange("p (h b) -> p h b", h=4),
                        pastb[:, t, :].unsqueeze(1).broadcast_to([128, 4, 8]), ALU.add,
                        reads=[self.psn(7), "pastb"], writes=["gs2"])
                for h in range(4):
                    self.P.op("dve", lambda e_, o=top8[:, h, :], i=gs2[:, h, :]: e_.max(out=o, in_=i),
                              reads=["gs2"], writes=["top8"])
                for h in range(4):
                    self.ts("dve", selm[:, h, :], gs2[:, h, :], top8[:, h, 2:3], ALU.is_ge,
                            reads=["gs2", "top8"], writes=["selm"])
                self.tt("dve", selm, selm, pastm[:, t, :].unsqueeze(1).broadcast_to([128, 4, 8]), ALU.mult,
                        reads=["selm", "pastm"], writes=["selm"])
                self.ts("dve", selmb, selm, -1.0, ALU.add, reads=["selm"], writes=["selmb"])
                pb = self.pst[7][:, :].bitcast(BF16)
                for h in range(4):
                    self.tr(pb[0:8, h * 128:(h + 1) * 128], selmb[:, h, :], self.identb,
                            reads=["selmb", "identb"], writes=[self.psn(7)])
                self.cp("act", sT, pb[0:8, 0:512].rearrange("p (h q) -> p h q", h=4),
                        reads=[self.psn(7)], writes=[sTr])
            kts = list(range(0, t + 1))
            for ii, kt in enumerate(kts):
                kS = ei % 2
                e = eb[ei % 3]
                er = ("eb", ei % 3)
                ei += 1
                ks = slice(kt * 128, (kt + 1) * 128)
                extra = []
                if kt < 2 * cur:
                    extra.append((e8[:, kt, :], sT.rearrange("p h q -> p (h q)"), ["e8", sTr]))
                if kt == t:
                    extra.append((self.identb, cbd.unsqueeze(1).broadcast_to([128, 4, 128]), ["cbd", "identb"]))
                for h in range(4):
                    self.mm(self.pst[kS][:, h * 128:(h + 1) * 128], kT[:, h, ks], qT[:, h, qs], h == 0,
                            (h == 3 and len(extra) == 0), reads=["kT", "qT"], writes=[self.psn(kS)])
                for xi, (l_, r_, rd) in enumerate(extra):
                    o_ = self.pst[kS][:, :]
                    if len(r_.shape) == 3:
                        o_ = o_.rearrange("p (h q) -> p h q", h=4)
                    self.mm(o_, l_, r_, False, xi == len(extra) - 1, reads=rd, writes=[self.psn(kS)])
                self.act(e, self.pst[kS][:, :], AF.Exp, scale=SCALE, reads=[self.psn(kS)], writes=[er])
                for h in range(4):
                    kb = ob[h // 2]
                    c0 = (h % 2) * 130
                    self.mm(self.pst[kb][:, c0:c0 + 129], e[:, h * 128:(h + 1) * 128], v1[:, h, kt, 0:129],
                            (ii == 0 and h % 2 == 0), (ii == len(kts) - 1),
                            reads=[er, "v1"], writes=[self.psn(kb)])
            for hh in range(2):
                self.cp("dve", Lall[:, hh * 2:hh * 2 + 2],
                        self.pst[ob[hh]][:, 0:260].rearrange("p (h c) -> p h c", h=2)[:, :, 128],
                        reads=[self.psn(ob[hh])], writes=["Lall"])
            self.ts("dve", Lall, Lall, 1e-30, ALU.max, reads=["Lall"], writes=["Lall"])
            self.P.op("dve", lambda e_, o=Lall, i=Lall: e_.reciprocal(out=o, in_=i), reads=["Lall"], writes=["Lall"])
            a_ = acc[t % 2]
            ar = ("acc", t % 2)
            for h in range(4):
                hs = slice(h * 128, (h + 1) * 128)
                c0 = (h % 2) * 130
                self.ts("dve", a_[:, hs], self.pst[ob[h // 2]][:, c0:c0 + 128], Lall[:, h:h + 1], ALU.mult,
                        reads=[self.psn(ob[h // 2]), "Lall"], writes=[ar])
            self.tt("pool", yo[t % 2], a_, szall[:, t, :], ALU.mult, reads=[ar, "szall"], writes=[("yo", t % 2)])
            self.dma(self.y_tm[qs, 1536:2048], yo[t % 2], reads=[("yo", t % 2)], writes=["y_tm"], slot=("yo", t % 2))
        self.phase_end()

    def group_mlstm(self, L):
        w = self.w_in[L]
        cst = self.cst
        prm = self.prm
        qT = self.carve([128, 4, S], BF16)
        kT = self.carve([128, 4, S], BF16)
        v1 = self.carve([128, 4, NT, 130], BF16)
        ifp = self.carve([128, NT, 8], F32)
        self.memset("pool", v1[:, :, :, 128:129], 1.0, writes=["v1"])
        kk = 0
        for dstT, c0, nm in ((qT, C_ML_QKV, "qT"), (kT, C_ML_QKV + 512, "kT")):
            if nm == "qT":
                slot = self.take_pref(lambda: self.wload(w[:, c0:c0 + 512], 512))
            else:
                slot = self.wload(w[:, c0:c0 + 512], 512)
            for h in range(4):
                for g in range(4):
                    k = kk % 6
                    kk += 1
                    self.proj_fm(slot, h * 128, g, k)
                    self.cp("act" if kk % 2 else "dve", dstT[:, h, g * 512:(g + 1) * 512], self.pst[k][:, :],
                            reads=[self.psn(k)], writes=[nm])
        slot = self.wload(w[:, C_ML_QKV + 1024:C_ML_QKV + 1536], 512)
        for t in range(NT):
            k = kk % 6
            kk += 1
            self.proj_tm(slot, 0, 512, t, k)
            self.cp("dve" if t % 2 else "act", v1[:, :, t, 0:128],
                    self.pst[k][:, :].rearrange("p (h d) -> p h d", h=4), reads=[self.psn(k)], writes=["v1"])
        slot = self.wload(w[:, C_ML_IF:C_ML_IF + 8], 8)
        for t in range(NT):
            for dc in range(16):
                self.mm(self.pst[6][:, t * 8:(t + 1) * 8], self.xT[:, dc, t * 128:(t + 1) * 128],
                        self.wb[slot][:, dc, 0:8], dc == 0, dc == 15,
                        reads=[("wb", slot), ("xT", t)], writes=[self.psn(6)])
        self.cp("dve", ifp, self.pst[6][:, 0:128].rearrange("p (t g) -> p t g", g=8), reads=[self.psn(6)], writes=["ifp"])
        oslot = self.wload(w[:, C_ML_O:C_ML_O + 512], 512)
        zslot = self.wload(w[:, C_ML_Z:C_ML_Z + 512], 512)
        bias8 = self.carve([128, 8], F32)
        self.dma(bias8[:, 0:4], prm["mlstm_i_bias"][L:L + 1, :].broadcast_to([128, 4]), reads=[], writes=["bias8"], slot="bias8a")
        self.dma(bias8[:, 4:8], prm["mlstm_f_bias"][L:L + 1, :].broadcast_to([128, 4]), reads=[], writes=["bias8"], slot="bias8b")
        ng = self.carve([128, 512], F32)
        self.dma(ng, prm["mlstm_norm_g"][L:L + 1, :].broadcast_to([128, 512]), reads=[], writes=["ng"], slot="ng")
        triu = self.load_const("triu", cst["triu"], [128, 128], F32)
        onesf = self.load_const("onesf", cst["onesf"], [128, 128], F32)
        caus = self.load_const("caus01", cst["caus01"], [128, 128], BF16)
        self.tt("dve", ifp, ifp, bias8.unsqueeze(1).broadcast_to([128, NT, 8]), ALU.add, reads=["ifp", "bias8"], writes=["ifp"])
        nl = self.carve([128, NT, 4], F32)
        nbS = self.carve([128, 2, NT, 4], F32)
        d1 = self.carve([128, NT, 4], F32)
        fack = self.carve([128, NT, 4], F32)
        rowf = self.carve([128, NT, 4], F32)
        edec = self.carve([128, NT, 4], F32)
        self.act(nl, ifp[:, :, 4:8], AF.Exp, scale=-1.0, reads=["ifp"], writes=["nl"])
        self.act(nl, nl, AF.Ln, bias=1.0, reads=["nl"], writes=["nl"])
        nlf = nl.rearrange("p t h -> p (t h)")
        self.mm(self.pst[6][:, 0:64], triu, nlf, True, True, reads=["triu", "nl"], writes=[self.psn(6)])
        self.mm(self.pst[6][:, 64:128], onesf, nlf, True, True, reads=["onesf", "nl"], writes=[self.psn(6)])
        self.cp("dve", nbS.rearrange("p a t h -> p (a t h)"), self.pst[6][:, 0:128], reads=[self.psn(6)], writes=["nbS"])
        self.tt("dve", d1, nbS[:, 0], nbS[:, 1], ALU.subtract, reads=["nbS"], writes=["d1"])
        self.tt("dve", fack, ifp[:, :, 0:4], d1, ALU.add, reads=["ifp", "d1"], writes=["fack"])
        self.act(fack, fack, AF.Exp, bias=float(np.log(SCALE)), reads=["fack"], writes=["fack"])
        self.act(rowf, d1, AF.Exp, scale=-1.0, reads=["d1"], writes=["rowf"])
        self.act(edec, nbS[:, 1], AF.Exp, scale=-1.0, reads=["nbS"], writes=["edec"])
        C = self.carve([128, 4, 130], F32)
        Cd = self.carve([128, 4, 130], F32)
        Cb = self.carve([128, 4, 130], BF16)
        Gm = [self.carve([128, 512], BF16) for _ in range(2)]
        ktm = [self.carve([128, 512], BF16) for _ in range(2)]
        Vh = [self.carve([128, 4, 130], BF16) for _ in range(2)]
        den = self.carve([128, 4], F32)
        rr = self.carve([128, 4], F32)
        so = [self.carve([128, 512], F32) for _ in range(2)]
        sz = [self.carve([128, 512], F32) for _ in range(2)]
        hg = [self.carve([128, 512], F32) for _ in range(2)]
        st6 = self.carve([128, 4, 6], F32)
        mv = self.carve([128, 4, 2], F32)
        rstd = self.carve([128, 4], F32)
        yo = [self.carve([128, 512], BF16) for _ in range(2)]
        for c in range(NT):
            qs = slice(c * 128, (c + 1) * 128)
            i2 = c % 2
            gS = c % 2
            for h in range(4):
                self.mm(self.pst[gS][:, h * 128:(h + 1) * 128], kT[:, h, qs], qT[:, h, qs], h == 0, h == 3,
                        reads=["kT", "qT"], writes=[self.psn(gS)])
            for h in range(4):
                hs = slice(h * 128, (h + 1) * 128)
                self.stt(Gm[i2][:, hs], self.pst[gS][:, hs], fack[:, c, h:h + 1], caus, ALU.mult, ALU.mult,
                         reads=[self.psn(gS), "fack", "caus01"], writes=[("Gm", i2)])
            pb = self.pst[2][:, :].bitcast(BF16)
            for h in range(4):
                self.tr(pb[:, h * 128:(h + 1) * 128], kT[:, h, qs], self.identb, reads=["kT", "identb"], writes=[self.psn(2)])
            self.cp("act", ktm[i2], pb[:, 0:512], reads=[self.psn(2)], writes=[("ktm", i2)])
            self.tt("pool", Vh[i2][:, :, 0:129], v1[:, :, c, 0:129],
                    fack[:, c, :].unsqueeze(2).broadcast_to([128, 4, 129]), ALU.mult,
                    reads=["v1", "fack"], writes=[("Vh", i2)])
            if c > 0:
                for h in range(4):
                    self.ts("dve", Cd[:, h, 0:129], C[:, h, 0:129], edec[:, c, h:h + 1], ALU.mult,
                            reads=["C", "edec"], writes=["Cd"])
                self.cp("pool", Cb[:, :, 0:129], Cd[:, :, 0:129], reads=["Cd"], writes=["Cb"])
            ob = (3, 4)
            for h in range(4):
                kb = ob[h // 2]
                c0 = (h % 2) * 130
                self.mm(self.pst[kb][:, c0:c0 + 129], Gm[i2][:, h * 128:(h + 1) * 128], v1[:, h, c, 0:129],
                        h % 2 == 0, c == 0, reads=[("Gm", i2), "v1"], writes=[self.psn(kb)])
                if c > 0:
                    self.mm(self.pst[kb][:, c0:c0 + 129], qT[:, h, qs], Cb[:, h, 0:129], False, True,
                            reads=["qT", "Cb"], writes=[self.psn(kb)])
            sbk = (5, 6)
            for h in range(4):
                kb = sbk[h // 2]
                c0 = (h % 2) * 130
                self.mm(self.pst[kb][:, c0:c0 + 129], ktm[i2][:, h * 128:(h + 1) * 128], Vh[i2][:, h, 0:129],
                        h % 2 == 0, True, reads=[("ktm", i2), ("Vh", i2)], writes=[self.psn(kb)])
            for j in range(2):
                pv = self.pst[sbk[j]][:, 0:260].rearrange("p (h c) -> p h c", h=2)[:, :, 0:129]
                if c == 0:
                    self.cp("dve", C[:, 2 * j:2 * j + 2, 0:129], pv, reads=[self.psn(sbk[j])], writes=["C"])
                else:
                    self.tt("dve", C[:, 2 * j:2 * j + 2, 0:129], Cd[:, 2 * j:2 * j + 2, 0:129], pv, ALU.add,
                            reads=[self.psn(sbk[j]), "Cd"], writes=["C"])
            for j in range(2):
                self.cp("dve", den[:, 2 * j:2 * j + 2],
                        self.pst[ob[j]][:, 0:260].rearrange("p (h c) -> p h c", h=2)[:, :, 128],
                        reads=[self.psn(ob[j])], writes=["den"])
            self.tt("dve", den, den, rowf[:, c, :], ALU.mult, reads=["den", "rowf"], writes=["den"])
            self.stt(den, den, -1.0, den, ALU.mult, ALU.max, reads=["den"], writes=["den"])
            self.ts("dve", den, den, 1.0, ALU.max, reads=["den"], writes=["den"])
            self.P.op("dve", lambda e_, o=den, i=den: e_.reciprocal(out=o, in_=i), reads=["den"], writes=["den"])
            self.tt("dve", rr, den, rowf[:, c, :], ALU.mult, reads=["den", "rowf"], writes=["rr"])
            self.proj_tm(oslot, 0, 512, c, 7)
            self.act(so[i2], self.pst[7][:, :], AF.Sigmoid, reads=[self.psn(7)], writes=[("so", i2)])
            self.proj_tm(zslot, 0, 512, c, 7)
            self.act(sz[i2], self.pst[7][:, :], AF.Silu, reads=[self.psn(7)], writes=[("sz", i2)])
            for h in range(4):
                hs = slice(h * 128, (h + 1) * 128)
                c0 = (h % 2) * 130
                self.stt(hg[i2][:, hs], self.pst[ob[h // 2]][:, c0:c0 + 128], rr[:, h:h + 1], so[i2][:, hs],
                         ALU.mult, ALU.mult, reads=[self.psn(ob[h // 2]), "rr", ("so", i2)], writes=[("hg", i2)])
            for h in range(4):
                hs = slice(h * 128, (h + 1) * 128)
                self.P.op("dve", lambda e_, o=st6[:, h, :], i=hg[i2][:, hs]: e_.bn_stats(out=o, in_=i),
                          reads=[("hg", i2)], writes=["st6"])
            for h in range(4):
                self.P.op("dve", lambda e_, o=mv[:, h, :], i=st6[:, h, :]: e_.bn_aggr(out=o, in_=i),
                          reads=["st6"], writes=["mv"])
            self.ts("dve", rstd, mv[:, :, 1], 1e-5, ALU.add, reads=["mv"], writes=["rstd"])
            self.act(rstd, rstd, AF.Ln, reads=["rstd"], writes=["rstd"])
            self.act(rstd, rstd, AF.Exp, scale=-0.5, reads=["rstd"], writes=["rstd"])
            for h in range(4):
                hs = slice(h * 128, (h + 1) * 128)
                self.ts("dve", hg[i2][:, hs], hg[i2][:, hs], mv[:, h, 0:1], ALU.subtract,
                        reads=[("hg", i2), "mv", "rstd"], writes=[("hg", i2)], s2=rstd[:, h:h + 1], op1=ALU.mult)
            self.tt("pool", hg[i2], hg[i2], ng, ALU.mult, reads=[("hg", i2), "ng"], writes=[("hg", i2)])
            self.tt("pool", yo[i2], hg[i2], sz[i2], ALU.mult, reads=[("hg", i2), ("sz", i2)], writes=[("yo", i2)])
            self.dma(self.y_tm[qs, 512:1024], yo[i2], reads=[("yo", i2)], writes=["y_tm"], slot=("yo", i2))
        self.run_pref()
        self.phase_end()

    def group_lru(self, L):
        w = self.w_in[L]
        prm = self.prm
        cw = self.carve([128, 4, 4], F32)
        cb = self.carve([128, 4], F32)
        gb = self.carve([128, 2, 4], F32)
        lam = self.carve([128, 4], F32)
        clam = self.carve([128, 4], F32)
        gwf = self.carve([128, 8, 128], F32)
        gwb = self.carve([128, 8, 128], BF16)
        for n in range(4):
            self.dma(cw[:, n, :], prm["lru_conv_w"][L][:, n * 128:(n + 1) * 128].rearrange("j c -> c j"),
                     reads=[], writes=["cw"], slot="cw", nc_ok=True)
        self.dma(cb, prm["lru_conv_b"][L].rearrange("(n c) -> c n", c=128), reads=[], writes=["cb"], slot="cb", nc_ok=True)
        for g_ in range(2):
            self.dma(gb[:, g_, :], prm["lru_gate_b"][L][g_].rearrange("(n c) -> c n", c=128),
                     reads=[], writes=["gb"], slot="gb", nc_ok=True)
        self.dma(lam, prm["lru_lambda"][L].rearrange("(n c) -> c n", c=128), reads=[], writes=["lam"], slot="lam", nc_ok=True)
        self.dma(gwf, prm["lru_gate_w"][L].rearrange("g n d e -> d (g n) e"), reads=[], writes=["gwf"], slot="gwf")
        self.cp("pool", gwb, gwf, reads=["gwf"], writes=["gwb"])
        self.act(clam, lam, AF.Exp, scale=-1.0, reads=["lam"], writes=["clam"])
        self.act(clam, clam, AF.Ln, bias=1.0, reads=["clam"], writes=["clam"])
        self.ts("dve", clam, clam, -8.0, ALU.mult, reads=["clam"], writes=["clam"])
        xslot = self.take_pref(lambda: self.wload(w[:, C_LRU_X:C_LRU_X + 512], 512))
        zslot = self.wload(w[:, C_LRU_Z:C_LRU_Z + 512], 512)
        xr = self.carve([128, S], F32)
        u = self.carve([128, S], F32)
        ub = self.carve([128, S], BF16)
        rr = self.carve([128, S], F32)
        ii_ = self.carve([128, S], F32)
        aa = self.carve([128, S], F32)
        szT = self.carve([128, S], BF16)
        yo = self.carve([128, S], BF16)
        kk = 0
        for n in range(4):
            for g in range(4):
                k = kk % 8
                kk += 1
                self.proj_fm(xslot, n * 128, g, k)
                self.cp("act" if g % 2 else "dve", xr[:, g * 512:(g + 1) * 512], self.pst[k][:, :],
                        reads=[self.psn(k)], writes=["xr"])
            for g in range(4):
                k = kk % 8
                kk += 1
                self.proj_fm(zslot, n * 128, g, k)
                self.act(szT[:, g * 512:(g + 1) * 512], self.pst[k][:, :], AF.Silu, reads=[self.psn(k)], writes=["szT"])
            self.ts("dve", u, xr, cw[:, n, 3:4], ALU.mult, reads=["xr", "cw", "cb"], writes=["u"],
                    s2=cb[:, n:n + 1], op1=ALU.add)
            for j in range(1, 4):
                self.stt(u[:, j:S], xr[:, 0:S - j], cw[:, n, 3 - j:4 - j], u[:, j:S], ALU.mult, ALU.add,
                         reads=["xr", "cw", "u"], writes=["u"])
            self.cp("act", ub, u, reads=["u"], writes=["ub"])
            for gi, dst, nm in ((0, rr, "rr"), (1, ii_, "ii")):
                for g in range(4):
                    k = kk % 8
                    kk += 1
                    self.mm(self.pst[k][:, :], gwb[:, gi * 4 + n, :], ub[:, g * 512:(g + 1) * 512], True, True,
                            reads=["gwb", "ub"], writes=[self.psn(k)])
                    self.act(dst[:, g * 512:(g + 1) * 512], self.pst[k][:, :], AF.Sigmoid, bias=gb[:, gi, n:n + 1],
                             reads=[self.psn(k), "gb"], writes=[nm])
            self.act(aa, rr, AF.Exp, scale=clam[:, n:n + 1], reads=["rr", "clam"], writes=["aa"])
            self.tt("dve", rr, aa, aa, ALU.mult, reads=["aa"], writes=["rr"])
            self.ts("dve", rr, rr, -1.0, ALU.mult, reads=["rr"], writes=["rr"], s2=1.0, op1=ALU.add)
            self.act(rr, rr, AF.Sqrt, reads=["rr"], writes=["rr"])
            self.tt("pool", ii_, ii_, u, ALU.mult, reads=["ii", "u"], writes=["ii"])
            self.tt("dve", ii_, ii_, rr, ALU.mult, reads=["ii", "rr"], writes=["ii"])
            self.P.op("dve", lambda e_, o=u, a=aa, b=ii_: e_.tensor_tensor_scan(
                out=o, data0=a, data1=b, initial=0.0, op0=ALU.mult, op1=ALU.add),
                reads=["aa", "ii"], writes=["u"], cost=4.4)
            self.tt("pool", yo, u, szT, ALU.mult, reads=["u", "szT"], writes=["yo"])
            self.dma(self.y_fm[n * 128:(n + 1) * 128, :], yo, reads=["yo"], writes=["y_fm"], slot="yoC")
        self.run_pref()
        self.phase_end()

    def load_wout(self, L):
        wo = self.prm["w_out"][L]
        for nb in range(4):
            self.wload(wo[:, nb * 512:(nb + 1) * 512], 512, dst=self.xT, dcol0=nb * 512,
                       dres=self.allxt() + [("xTw", nb)])
        return True

    def out_proj(self, L, xsrc, dst):
        prm = self.prm
        self.take_pref(lambda: self.load_wout(L))
        self.run_pref()
        lng = self.carve([128, D], F32)
        lnb = self.carve([128, D], F32)
        self.dma(lng, prm["ln_g"][L:L + 1, :].broadcast_to([128, D]), reads=[], writes=["lng"], slot="lng")
        self.dma(lnb, prm["ln_b"][L:L + 1, :].broadcast_to([128, D]), reads=[], writes=["lnb"], slot="lnb")
        ya = [self.carve([128, D], BF16) for _ in range(2)]
        yT = [self.carve([128, 16, 128], BF16) for _ in range(2)]
        xres = [self.carve([128, D], F32) for _ in range(2)]
        tb = [self.carve([128, D], F32) for _ in range(2)]
        st6 = self.carve([128, 4, 6], F32)
        mv = self.carve([128, 2], F32)
        rstd = self.carve([128, 1], F32)
        yfv = self.y_fm.rearrange("(n f) t -> f n t", f=128)
        for t in range(NT):
            qs = slice(t * 128, (t + 1) * 128)
            i2 = t % 2
            self.dma(ya[i2], self.y_tm[qs, :], reads=["y_tm"], writes=[("ya", i2)], slot=("ya", i2))
            self.dma(yT[i2][:, 8:12, :], yfv[:, :, qs], reads=["y_fm"], writes=[("yT", i2)], slot=("yTf", i2))
            self.dma(xres[i2], xsrc[qs, :], reads=["xsrc"], writes=[("xres", i2)], slot=("xres", i2))
            for gi, fcs in enumerate(((0, 1, 2, 3, 4, 5, 6, 7), (12, 13, 14, 15))):
                kbk = 4 + gi
                pb = self.pst[kbk][:, :].bitcast(BF16)
                for j, fc in enumerate(fcs):
                    self.tr(pb[:, j * 128:(j + 1) * 128], ya[i2][:, fc * 128:(fc + 1) * 128], self.identb,
                            reads=[("ya", i2), "identb"], writes=[self.psn(kbk)])
                n_ = len(fcs)
                self.cp("act" if gi == 0 else "dve", yT[i2][:, fcs[0]:fcs[0] + n_, :],
                        pb[:, 0:n_ * 128].rearrange("p (c q) -> p c q", c=n_), reads=[self.psn(kbk)], writes=[("yT", i2)])
            for nb in range(4):
                k = nb
                for fc in range(16):
                    self.mm(self.pst[k][:, :], yT[i2][:, fc, :], self.xT[:, fc, nb * 512:(nb + 1) * 512], fc == 0, fc == 15,
                            reads=[("yT", i2), ("xTw", nb)], writes=[self.psn(k)])
                ns = slice(nb * 512, (nb + 1) * 512)
                self.stt(tb[i2][:, ns], xres[i2][:, ns], ALPHA, self.pst[k][:, :], ALU.mult, ALU.add,
                         reads=[("xres", i2), self.psn(k)], writes=[("tb", i2)])
                self.P.op("dve", lambda e_, o=st6[:, nb, :], i=tb[i2][:, ns]: e_.bn_stats(out=o, in_=i),
                          reads=[("tb", i2)], writes=["st6o"])
            self.P.op("dve", lambda e_, o=mv, i=st6.rearrange("p a b -> p (a b)"): e_.bn_aggr(out=o, in_=i),
                      reads=["st6o"], writes=["mvo"])
            self.ts("dve", rstd, mv[:, 1:2], 1e-5, ALU.add, reads=["mvo"], writes=["rstdo"])
            self.act(rstd, rstd, AF.Ln, reads=["rstdo"], writes=["rstdo"])
            self.act(rstd, rstd, AF.Exp, scale=-0.5, reads=["rstdo"], writes=["rstdo"])
            self.ts("dve", tb[i2], tb[i2], mv[:, 0:1], ALU.subtract, reads=[("tb", i2), "mvo", "rstdo"],
                    writes=[("tb", i2)], s2=rstd[:, 0:1], op1=ALU.mult)
            self.tt("pool", tb[i2], tb[i2], lng, ALU.mult, reads=[("tb", i2), "lng"], writes=[("tb", i2)])
            self.tt("pool", tb[i2], tb[i2], lnb, ALU.add, reads=[("tb", i2), "lnb"], writes=[("tb", i2)])
            self.dma(dst[qs, :], tb[i2], reads=[("tb", i2)], writes=["dst%d" % L], slot=("tbo", i2))
        self.phase_end()


def make_consts():
    bf = ml_dtypes.bfloat16
    c = {}
    c["identf"] = np.eye(128, dtype=np.float32)
    c["identb"] = np.eye(128, dtype=np.float32).astype(bf)
    half = 16
    inv_freq = np.power(np.float32(500000.0), -np.arange(half, dtype=np.float32) * np.float32(2.0 / 32))
    ang = np.arange(S, dtype=np.float32)[:, None] * inv_freq[None, :]
    cos = np.cos(ang).astype(np.float32).T
    sin = np.sin(ang).astype(np.float32).T
    c["cosT"] = np.concatenate([cos, cos], 0)
    c["sinT"] = np.concatenate([sin, sin], 0)
    rot = np.zeros((32, 32), np.float32)
    for m in range(16):
        rot[m + 16, m] = -1.0
        rot[m, m + 16] = 1.0
    c["rotP"] = rot
    n = np.arange(128)[:, None]
    q = np.arange(S)[None, :]
    c["cmpb"] = np.where((16 * n + 31 <= q) & (n < 127), 0.0, NEGB).astype(bf)
    esel = np.zeros((32, NT, 128), np.float32)
    for kt in range(NT):
        for k in range(128):
            esel[2 * kt + k // 64, kt, k] = -NEGB
    c["esel"] = esel.astype(bf)
    kl = np.arange(128)[:, None]
    ql = np.arange(128)[None, :]
    c["cbd"] = np.where(kl <= ql, 0.0, NEGB).astype(bf)
    c["cbw"] = np.where(kl > ql, 0.0, NEGB).astype(bf)
    cs = np.arange(128) * 16
    ss = np.arange(32) * 64
    ovl = ((cs[:, None] < ss[None, :] + 64) & (cs[:, None] + 32 > ss[None, :])).astype(np.float32)
    ovl[127, :] = 0.0
    o34 = np.zeros((128, 34), np.float32)
    o34[:, :32] = ovl
    o34[:127, 32] = 1.0
    c["ovl"] = o34.astype(bf)
    t = np.arange(S)
    cur = t // 64
    j = np.arange(32)[None, :]
    forced = (j == 0) | (j == cur[:, None]) | (j == cur[:, None] - 1)
    valid = j <= cur[:, None]
    m1 = (valid & ~forced).astype(np.float32)
    m2 = np.where(forced, 1e9, np.where(valid, 0.0, -1e30)).astype(np.float32)
    c["m1"] = np.ascontiguousarray(m1.reshape(NT, 128, 32).transpose(1, 0, 2))
    c["m2"] = np.ascontiguousarray(m2.reshape(NT, 128, 32).transpose(1, 0, 2))
    e8 = np.zeros((8, NT, 128), np.float32)
    for kt in range(NT):
        e8[kt // 2, kt, :] = -NEGB
    c["e8"] = e8.astype(bf)
    curb = (np.arange(NT) // 2)[:, None]
    jb = np.arange(8)[None, :]
    past = (jb < curb)
    c["pastb"] = np.ascontiguousarray(np.broadcast_to(np.where(past, 0.0, -1e30).astype(np.float32)[None], (128, NT, 8)))
    c["pastm"] = np.ascontiguousarray(np.broadcast_to(past.astype(np.float32)[None], (128, NT, 8)))
    c["triu"] = (kl <= ql).astype(np.float32)
    c["onesf"] = np.ones((128, 128), np.float32)
    c["caus01"] = (kl <= ql).astype(np.float32).astype(bf)
    return c


PARAM_NAMES = ["w_in", "nsa_cmp_w1", "nsa_cmp_w2", "nsa_cmp_pe", "mlstm_i_bias", "mlstm_f_bias",
               "mlstm_norm_g", "lru_conv_w", "lru_conv_b", "lru_gate_w", "lru_gate_b", "lru_lambda",
               "w_out", "ln_g", "ln_b"]
PARAM_SHAPES = {
    "w_in": [2, D, INW], "nsa_cmp_w1": [2, 2, 4096, 128], "nsa_cmp_w2": [2, 2, 128, 128],
    "nsa_cmp_pe": [2, 2, 32, 128], "mlstm_i_bias": [2, 4], "mlstm_f_bias": [2, 4],
    "mlstm_norm_g": [2, 512], "lru_conv_w": [2, 4, 512], "lru_conv_b": [2, 512],
    "lru_gate_w": [2, 2, 4, 128, 128], "lru_gate_b": [2, 2, 512], "lru_lambda": [2, 512],
    "w_out": [2, D, D], "ln_g": [2, D], "ln_b": [2, D],
}


def build(layers=(0, 1), groups="ABCD", do_out=True, dbg=False, consts=None):
    nc = bass.Bass("TRN2", target_bir_lowering=False)
    kb = KB(nc, dbg=dbg)
    kb.consts = {}
    x = nc.dram_tensor("x", [S, D], F32, kind="ExternalInput").ap()
    prm = {n: nc.dram_tensor(n, PARAM_SHAPES[n], F32, kind="ExternalInput").ap() for n in PARAM_NAMES}
    kb.cst = {}
    for n, a in consts.items():
        dt = BF16 if a.dtype == ml_dtypes.bfloat16 else F32
        kb.cst[n] = nc.dram_tensor("c_" + n, list(a.shape), dt, kind="ExternalInput").ap()
    out = nc.dram_tensor("out", [S, D], F32, kind="ExternalOutput").ap()
    okind = "ExternalOutput" if dbg else "Internal"
    kb.y_tm = nc.dram_tensor("y_tm", [S, D], BF16, kind=okind).ap()
    kb.y_fm = nc.dram_tensor("y_fm", [512, S], BF16, kind=okind).ap()
    x1 = nc.dram_tensor("x1s", [S, D], F32, kind="Internal").ap()
    if dbg:
        kb.dbg_ya = nc.dram_tensor("dbg_ya", [S, 512], F32, kind="ExternalOutput").ap()
    kb.w_in = [prm["w_in"][l] for l in range(2)]
    kb.w_cmp1 = [prm["nsa_cmp_w1"][l] for l in range(2)]
    kb.w_cmp2 = [prm["nsa_cmp_w2"][l] for l in range(2)]
    kb.w_pe = [prm["nsa_cmp_pe"][l] for l in range(2)]
    kb.prm = prm
    with kb.st:
        kb.setup()
        kb.dma(kb.identf, kb.cst["identf"], reads=[], writes=["identf"], slot="identf")
        kb.dma(kb.identb, kb.cst["identb"], reads=[], writes=["identb"], slot="identb")
        srcs = [x, x1]
        dsts = [x1, out]
        if (groups == "ABCD" and do_out and FLAG_PREF) or os.environ.get("K_FORCEPREF"):
            def mk(L):
                w = kb.w_in[L]
                return [lambda: kb.wload(w[:, C_NSA_Q:C_NSA_Q + 512], 512),
                        lambda: kb.wload(w[:, C_ML_QKV:C_ML_QKV + 512], 512),
                        lambda: kb.wload(w[:, C_LRU_X:C_LRU_X + 512], 512),
                        lambda: kb.wload(w[:, C_MB_QKV:C_MB_QKV + 512], 512),
                        lambda: kb.load_wout(L)]
            kb.pref_q = [(f if (FLAG_PREF >> i) & 1 else (lambda: None)) for L in layers for i, f in enumerate(mk(L))]
        if len(layers) == 1:
            srcs = {layers[0]: x}
            dsts = {layers[0]: out}
        for L in layers:
            kb.phase0(srcs[L])
            if "A" in groups:
                kb.group_nsa(L)
            if "B" in groups:
                kb.group_mlstm(L)
            if "C" in groups:
                kb.group_lru(L)
            if "D" in groups:
                kb.group_moba(L)
            if do_out:
                kb.out_proj(L, srcs[L], dsts[L])
        kb.P.emit()
    return nc


_CACHE = {}


def _get_nc(layers=(0, 1)):
    key = tuple(layers)
    if key not in _CACHE:
        consts = make_consts()
        _CACHE[key] = (build(layers=layers, consts=consts), consts)
    return _CACHE[key]


def kernel(**inputs):
    nc, consts = _get_nc((0, 1))
    x = np.ascontiguousarray(np.asarray(inputs["x"], dtype=np.float32))
    B = x.shape[0]
    base = {n: np.ascontiguousarray(np.asarray(inputs[n], dtype=np.float32)) for n in PARAM_NAMES}
    for n, a in consts.items():
        base["c_" + n] = a
    in_maps = []
    for b in range(B):
        m = dict(base)
        m["x"] = x[b]
        in_maps.append(m)
    res = run_bass_kernel_spmd(nc, in_maps, core_ids=list(range(B)))
    return np.stack([np.asarray(r["out"], dtype=np.float32) for r in res.results], axis=0)
```

```python
import contextlib
import numpy as np
import ml_dtypes
import concourse.bass as bass
import concourse.mybir as mybir
from concourse.alu_op_type import AluOpType as ALU
from concourse.bass_utils import run_bass_kernel_spmd

F32 = mybir.dt.float32
BF16 = mybir.dt.bfloat16
AF = mybir.ActivationFunctionType
AX = mybir.AxisListType
ENGS = ("pe", "act", "dve", "pool", "sp")
import os
FLAG_SPW = int(os.environ.get("K_SPW", "12"))
FLAG_CASTACT = int(os.environ.get("K_CASTACT", "0"))
FLAG_PREF = int(os.environ.get("K_PREF", "31"))
FLAG_DMACAST = int(os.environ.get("K_DMACAST", "1"))
FLAG_STRICT = int(os.environ.get("K_STRICT", "0"))

S = 2048
D = 2048
NT = 16
INW = 7444
NEGB = -30000.0
SCALE = 128 ** -0.5
ALPHA = float((2 * 2) ** 0.25)

C_NSA_Q = 0
C_NSA_KV = 512
C_NSA_G = 1280
C_NSA_Z = 1292
C_ML_QKV = 1804
C_ML_IF = 3340
C_ML_O = 3348
C_ML_Z = 3860
C_LRU_X = 4372
C_LRU_Z = 4884
C_MB_QKV = 5396
C_MB_Z = 6932


class Prog:
    def __init__(self, nc):
        self.nc = nc
        self.ops = []
        self.last_w = {}
        self.readers = {}
        self.slot_count = {}
        self.slot_order = []
        self.eng_h = {"pe": nc.tensor, "act": nc.scalar, "dve": nc.vector,
                      "pool": nc.gpsimd, "sp": nc.sync}

    def barrier(self):
        if not hasattr(self, "bar"):
            self.bar = {}
            self.since = {e: [] for e in ENGS}
        n1 = {}
        mk = getattr(self, "markers", {})
        for e in ENGS:
            if e == "pe":
                continue
            prev = list(self.since[e])
            fn = mk.get(e, (lambda h: h.nop()))
            n1[e] = self.op(e, fn, order_extra=prev, cost=0.1, _nosince=True)
        for e in ENGS:
            prev = list(self.since[e]) if e == "pe" else [n1[e]]
            self.bar[e] = self.op(e, lambda h: h.nop(), extra=[n1[f] for f in n1 if f != e],
                                  order_extra=prev, cost=0.05, _nosince=True)
            self.since[e] = []

    def op(self, eng, fn, reads=(), writes=(), slot=None, extra=(), cost=0.3, order_extra=(),
           _nosince=False):
        i = len(self.ops)
        raw = set(extra)
        oth = set(order_extra)
        if not hasattr(self, "since"):
            self.bar = {}
            self.since = {e: [] for e in ENGS}
        if not _nosince:
            self.since[eng].append(i)
        for r in reads:
            w = self.last_w.get(r)
            if w is not None:
                raw.add(w)
        for r in writes:
            w = self.last_w.get(r)
            if w is not None:
                oth.add(w)
            for rd in self.readers.get(r, ()):
                oth.add(rd)
        bi = getattr(self, "bar", {}).get(eng)
        if bi is not None:
            oth.add(bi)
        raw.discard(i)
        oth.discard(i)
        o = dict(eng=eng, fn=fn, raw=raw, oth=oth - raw, slot=slot, i=i, cost=cost)
        if slot is not None:
            if slot not in self.slot_count:
                self.slot_count[slot] = 0
                self.slot_order.append(slot)
            self.slot_count[slot] += 1
            o["slot_n"] = self.slot_count[slot]
        self.ops.append(o)
        for r in reads:
            self.readers.setdefault(r, []).append(i)
        for r in writes:
            self.last_w[r] = i
            self.readers[r] = []
        return i

    def schedule(self, window=int(os.environ.get('K_WIN', '60')), lat=0.25):
        ops = self.ops
        n = len(ops)
        succ = [[] for _ in range(n)]
        for o in ops:
            order = o["raw"] | o["oth"]
            sem = set()
            for d in order:
                p = ops[d]
                if p["slot"] is not None or o["slot"] is not None:
                    sem.add(d)
                elif p["eng"] != o["eng"]:
                    sem.add(d)
                elif d in o["raw"] and o["eng"] != "pe":
                    sem.add(d)
                elif FLAG_STRICT and p["fn"] is not None and not p.get("isnop"):
                    sem.add(d)
            o["order"], o["deps"] = order, sem
            o["nun"] = len(order)
            o["ready"] = 0.0
            for d in order:
                succ[d].append(o["i"])
        pend = {e: [o["i"] for o in ops if o["eng"] == e] for e in ENGS}
        head = {e: 0 for e in ENGS}
        done = [False] * n
        fin = [0.0] * n
        clock = {e: 0.0 for e in ENGS}
        streams = {e: [] for e in ENGS}
        left = n
        while left:
            best = None
            for e in ENGS:
                lst = pend[e]
                h = head[e]
                while h < len(lst) and done[lst[h]]:
                    h += 1
                head[e] = h
                cnt = 0
                j = h
                w = FLAG_SPW if e == "sp" else window
                while j < len(lst) and cnt < w:
                    idx = lst[j]
                    j += 1
                    if done[idx]:
                        continue
                    cnt += 1
                    o = ops[idx]
                    if o["nun"] == 0:
                        st = max(clock[e], o["ready"])
                        key = (st, idx)
                        if best is None or key < best[0]:
                            best = (key, e, idx)
            assert best is not None, "scheduler deadlock"
            (st, idx), e, _ = best
            o = ops[idx]
            done[idx] = True
            left -= 1
            if o["slot"] is not None:
                clock[e] = st + 0.06
            else:
                clock[e] = st + o["cost"]
            fin[idx] = st + o["cost"]
            streams[e].append(idx)
            for s_ in succ[idx]:
                so = ops[s_]
                so["nun"] -= 1
                r = fin[idx] + (lat if (so["eng"] != e or o["slot"] is not None) else 0.0)
                if r > so["ready"]:
                    so["ready"] = r
        self.sim_time = max(fin) if n else 0.0
        return streams

    def emit(self, final_wait_eng="sp"):
        nc = self.nc
        ops = self.ops
        streams = self.schedule()
        pos = {}
        for e in ENGS:
            for k_, idx in enumerate(streams[e]):
                pos[idx] = k_
        for o in ops:
            best = {}
            for d in o["deps"]:
                p = ops[d]
                if p["slot"] is not None:
                    key = ("s", p["slot"])
                    val = p["slot_n"]
                else:
                    key = ("t", p["eng"])
                    val = pos[d]
                if key not in best or val > best[key][0]:
                    best[key] = (val, d)
            o["deps"] = {d for (_, d) in best.values()}
        has_dep = [False] * len(ops)
        for o in ops:
            for d in o["deps"]:
                has_dep[d] = True
        tick = {e: 0 for e in ENGS}
        for e in ENGS:
            for idx in streams[e]:
                o = ops[idx]
                if o["slot"] is None and has_dep[idx]:
                    tick[e] += 1
                    o["tick"] = tick[e]
        with contextlib.ExitStack() as st:
            tsem = {e: st.enter_context(nc.semaphore("tk_" + e)) for e in ENGS}
            ssem = {s: st.enter_context(nc.semaphore("sl_%d" % k))
                    for k, s in enumerate(self.slot_order)}
            block = st.enter_context(nc.Block())
            total_slots = dict(self.slot_count)

            self.trace_ev = {e: [] for e in ENGS}

            def stream(eng):
                h = self.eng_h[eng]
                waited = {}
                ev = self.trace_ev[eng]
                for idx in streams[eng]:
                    o = ops[idx]
                    need = {}
                    for d in o["deps"]:
                        p = ops[d]
                        if p["slot"] is not None:
                            key = ("s", p["slot"])
                            val = 16 * p["slot_n"]
                            sem = ssem[p["slot"]]
                        else:
                            key = ("t", p["eng"])
                            val = p["tick"]
                            sem = tsem[p["eng"]]
                        if val > need.get(key, (None, 0))[1]:
                            need[key] = (sem, val)
                    for key, (sem, val) in need.items():
                        if waited.get(key, 0) >= val:
                            continue
                        h.wait_ge(sem, val)
                        waited[key] = val
                        ev.append(("w", key, val, idx))
                    ins = o["fn"](h)
                    if o["slot"] is not None:
                        ins.then_inc(ssem[o["slot"]], 16)
                        ev.append(("i", ("s", o["slot"]), 16, idx))
                    elif has_dep[idx]:
                        ins.then_inc(tsem[eng], 1)
                        ev.append(("i", ("t", eng), 1, idx))
                if eng == final_wait_eng:
                    for s, n_ in total_slots.items():
                        h.wait_ge(ssem[s], 16 * n_)
                    for e in ENGS:
                        if tick[e] > 0:
                            h.wait_ge(tsem[e], tick[e])

            @block.tensor
            def _(e):
                stream("pe")

            @block.scalar
            def _(e):
                stream("act")

            @block.vector
            def _(e):
                stream("dve")

            @block.gpsimd
            def _(e):
                stream("pool")

            @block.sync
            def _(e):
                stream("sp")
        return tick


class K:
    def __init__(self, nc):
        self.nc = nc
        self.P = Prog(nc)
        self.st = contextlib.ExitStack()
        self.ps_rr = 0

    def sb(self, name, shape, dt):
        return self.st.enter_context(self.nc.sbuf_tensor(name, list(shape), dt)).ap()

    def ps(self, name, shape, dt):
        return self.st.enter_context(self.nc.psum_tensor(name, list(shape), dt)).ap()

    @staticmethod
    def nf(ap):
        return int(np.prod(ap.shape[1:]))

    def mm(self, out, lhsT, rhs, start, stop, reads, writes):
        n = self.nf(rhs)
        c = max(64, n) / 2400.0 * (4.0 if rhs.dtype == F32 else 1.0) + 0.01
        self.P.op("pe", lambda e: e.matmul(out, lhsT=lhsT, rhs=rhs, start=start, stop=stop),
                  reads=reads, writes=writes, cost=c)

    def tr(self, out, in_, ident, reads, writes):
        self.P.op("pe", lambda e: e.transpose(out, in_, ident), reads=reads, writes=writes, cost=0.07)

    def act(self, out, in_, func, reads, writes, bias=None, scale=None, eng="act"):
        kw = {}
        if bias is not None:
            kw["bias"] = bias
        if scale is not None:
            kw["scale"] = scale
        self.P.op(eng, lambda e: e.activation(out=out, in_=in_, func=func, **kw),
                  reads=reads, writes=writes, cost=(224 + self.nf(out)) / 1200.0)

    def ecost(self, eng, out):
        n = self.nf(out)
        if eng == "act":
            return (224 + n) / 1200.0
        if eng == "pool":
            return 0.25 + n * 2.5e-3
        return 0.12 + n / 960.0

    def cp(self, eng, out, in_, reads, writes):
        c = self.ecost(eng, out)
        if eng == "act":
            self.P.op("act", lambda e: e.activation(out=out, in_=in_, func=AF.Copy),
                      reads=reads, writes=writes, cost=c)
        else:
            self.P.op(eng, lambda e: e.tensor_copy(out=out, in_=in_), reads=reads, writes=writes, cost=c)

    def tt(self, eng, out, in0, in1, op, reads, writes):
        self.P.op(eng, lambda e: e.tensor_tensor(out=out, in0=in0, in1=in1, op=op),
                  reads=reads, writes=writes, cost=self.ecost(eng, out))

    def ts(self, eng, out, in0, s1, op0, reads, writes, s2=None, op1=None):
        c = self.ecost(eng, out)
        if op1 is None:
            self.P.op(eng, lambda e: e.tensor_scalar(out=out, in0=in0, scalar1=s1, scalar2=None, op0=op0),
                      reads=reads, writes=writes, cost=c)
        else:
            self.P.op(eng, lambda e: e.tensor_scalar(out=out, in0=in0, scalar1=s1, scalar2=s2,
                                                     op0=op0, op1=op1),
                      reads=reads, writes=writes, cost=c)

    def stt(self, out, in0, scalar, in1, op0, op1, reads, writes):
        self.P.op("dve", lambda e: e.scalar_tensor_tensor(out=out, in0=in0, scalar=scalar, in1=in1,
                                                          op0=op0, op1=op1),
                  reads=reads, writes=writes, cost=self.ecost("dve", out))

    def dma(self, out, in_, reads, writes, slot, eng="sp", nc_ok=False):
        nbytes = self.nf(out) * out.shape[0] * (4 if out.dtype == F32 else 2)
        c = 2.0 + nbytes / 150e3
        if nc_ok:
            self.P.op(eng, lambda e: e.dma_start(out=out, in_=in_, allow_slow_non_contiguous=True),
                      reads=reads, writes=writes, slot=slot, cost=c)
        else:
            self.P.op(eng, lambda e: e.dma_start(out=out, in_=in_), reads=reads, writes=writes, slot=slot, cost=c)

    def memset(self, eng, ap, val, writes):
        self.P.op(eng, lambda e: e.memset(ap, val), reads=(), writes=writes, cost=self.ecost(eng, ap))


ARENA_BYTES = 106 * 1024


class KB(K):
    def __init__(self, nc, dbg=False):
        super().__init__(nc)
        self.dbg = dbg
        self.aoff = 0
        self.cnt = 0

    def setup(self):
        self.xT = self.sb("xT", [128, 16, S], BF16)
        self.wb = [self.sb("wb%d" % i, [128, 16, 512], BF16) for i in range(2)]
        self.identf = self.sb("identf", [128, 128], F32)
        self.identb = self.sb("identb", [128, 128], BF16)
        self.arena = self.sb("arena", [128, ARENA_BYTES // 2], BF16)
        bscr = self.sb("bscr", [128, 4], F32)
        self.P.markers = {
            "act": lambda h: h.activation(out=bscr[:, 0:1], in_=bscr[:, 3:4], func=AF.Copy),
            "dve": lambda h: h.tensor_copy(out=bscr[:, 1:2], in_=bscr[:, 3:4]),
            "pool": lambda h: h.tensor_copy(out=bscr[:, 2:3], in_=bscr[:, 3:4]),
        }
        self.pst = [self.ps("psb%d" % i, [128, 512], F32) for i in range(8)]
        self.ws_rr = 0
        self.wb_rr = 0

    def psn(self, k):
        return "ps%d" % k

    def carve(self, shape, dt):
        esz = 4 if dt == F32 else 2
        n = int(np.prod(shape[1:]))
        nbytes = (n * esz + 63) // 64 * 64
        off = self.aoff
        self.aoff += nbytes
        assert self.aoff <= ARENA_BYTES, ("arena overflow", self.aoff)
        v = self.arena[:, off // 2:(off + n * esz) // 2]
        if dt == F32:
            v = v.bitcast(F32)
        if len(shape) == 3:
            v = v.rearrange("p (a b) -> p a b", a=shape[1])
        elif len(shape) == 4:
            v = v.rearrange("p (a b c) -> p a b c", a=shape[1], b=shape[2])
        if shape[0] < 128:
            v = v[0:shape[0]]
        return v

    def phase_end(self):
        self.P.barrier()
        self.aoff = 0

    def run_pref(self):
        q = getattr(self, "pref_q", None)
        if q:
            self.pref = q.pop(0)()

    def take_pref(self, fn):
        p = getattr(self, "pref", None)
        self.pref = None
        return p if p is not None else fn()

    def uid(self, s):
        self.cnt += 1
        return "%s_%d" % (s, self.cnt)

    def const_in(self, name, arr):
        dt = BF16 if arr.dtype == ml_dtypes.bfloat16 else F32
        t = self.nc.dram_tensor(name, list(arr.shape), dt, kind="ExternalInput").ap()
        self.consts[name] = arr
        return t

    def load_const(self, name, dram_ap, shape, dt):
        t = self.carve(shape, dt)
        self.dma(t, dram_ap, reads=[], writes=[name], slot="c_" + name)
        return t

    def wload(self, wsrc, ncols, dst=None, dcol0=0, dres=None):
        if dst is None:
            slot = self.wb_rr
            self.wb_rr ^= 1
            dst = self.wb[slot]
            dres = ("wb", slot)
        else:
            slot = None
        if FLAG_DMACAST:
            dl = dres if isinstance(dres, list) else [dres]
            tag = "xT" if slot is None else "wb%d" % slot
            for g in range(4):
                src = wsrc[g * 512:(g + 1) * 512, :].rearrange("(c p) n -> p c n", p=128)
                self.dma(dst[:, g * 4:(g + 1) * 4, dcol0:dcol0 + ncols], src, reads=[], writes=dl,
                         slot=("wcast", tag, g), eng="pool")
            return slot
        for g in range(4):
            s = self.ws_rr
            self.ws_rr ^= 1
            stg = self.wstg[s]
            src = wsrc[g * 512:(g + 1) * 512, :].rearrange("(c p) n -> p c n", p=128)
            self.dma(stg[:, :, 0:ncols], src, reads=[], writes=[("wstg", s)], slot=("wstg", s))
            self.cp("act" if (g % 2 == 0 and FLAG_CASTACT) else "pool", dst[:, g * 4:(g + 1) * 4, dcol0:dcol0 + ncols], stg[:, :, 0:ncols],
                    reads=[("wstg", s)], writes=[dres] if not isinstance(dres, list) else dres)
        return slot

    def allxt(self):
        return [("xT", t) for t in range(NT)]

    def proj_fm(self, slot, c0, g, k):
        for dc in range(16):
            self.mm(self.pst[k][:, :], self.wb[slot][:, dc, c0:c0 + 128],
                    self.xT[:, dc, g * 512:(g + 1) * 512], dc == 0, dc == 15,
                    reads=[("wb", slot)] + [("xT", t) for t in range(4 * g, 4 * g + 4)],
                    writes=[self.psn(k)])

    def proj_tm(self, slot, c0, ncols, t, k):
        for dc in range(16):
            self.mm(self.pst[k][:, 0:ncols], self.xT[:, dc, t * 128:(t + 1) * 128],
                    self.wb[slot][:, dc, c0:c0 + ncols], dc == 0, dc == 15,
                    reads=[("wb", slot), ("xT", t)], writes=[self.psn(k)])

    def phase0(self, xsrc):
        if getattr(self, "pref", None) is None:
            self.run_pref()
        save = self.aoff
        self.aoff = ARENA_BYTES - 2 * 8192 - 2 * 4096
        xin = [self.carve([128, 2048], F32) for _ in range(2)]
        xbf = [self.carve([128, 2048], BF16) for _ in range(2)]
        self.aoff = save
        for t in range(NT):
            b = xin[t % 2]
            bb = xbf[t % 2]
            self.dma(b, xsrc[t * 128:(t + 1) * 128, :], reads=[], writes=[("xin", t % 2)],
                     slot=("xin", t % 2))
            self.cp("act" if t % 2 == 0 else "dve", bb, b, reads=[("xin", t % 2)], writes=[("xbf", t % 2)])
            for g in range(2):
                k = (t * 2 + g) % 8
                pb = self.pst[k][:, :].bitcast(BF16)
                for j in range(8):
                    dc = g * 8 + j
                    self.tr(pb[:, j * 128:(j + 1) * 128], bb[:, dc * 128:(dc + 1) * 128],
                            self.identb, reads=[("xbf", t % 2), "identb"], writes=[self.psn(k)])
                self.cp("dve" if t % 2 == 0 else "act",
                        self.xT[:, g * 8:(g + 1) * 8, t * 128:(t + 1) * 128],
                        pb.rearrange("p (c q) -> p c q", c=8),
                        reads=[self.psn(k)], writes=[("xT", t)])
        self.p0_top = ARENA_BYTES - 2 * 8192 - 2 * 4096

    def rope_setup(self):
        self.cosT = self.load_const("cosT", self.cst["cosT"], [32, S], F32)
        self.sinT = self.load_const("sinT", self.cst["sinT"], [32, S], F32)
        self.rotP = self.load_const("rotP", self.cst["rotP"], [32, 32], F32)
        self.qf = [self.carve([128, 512], F32) for _ in range(2)]
        self.rt = [self.carve([32, 512], F32) for _ in range(2)]
        self.rope_i = 0

    def rope_evac(self, k, dst, g, dres, krot=7):
        i = self.rope_i % 2
        self.rope_i += 1
        qf = self.qf[i]
        t1, t2 = self.rt
        rq, r1, r2 = ("qf", i), "rt1", "rt2"
        sl = slice(g * 512, (g + 1) * 512)
        self.cp("act", qf[0:32, :], self.pst[k][0:32, :], reads=[self.psn(k)], writes=[rq])
        self.mm(self.pst[krot][0:32, :], self.rotP, qf[0:32, :], True, True,
                reads=[rq, "rotP"], writes=[self.psn(krot)])
        self.tt("dve", t1, qf[0:32, :], self.cosT[:, sl], ALU.mult, reads=[rq, "cosT"], writes=[r1])
        self.tt("dve", t2, self.pst[krot][0:32, :], self.sinT[:, sl], ALU.mult,
                reads=[self.psn(krot), "sinT"], writes=[r2])
        self.tt("dve", dst[0:32, :], t1, t2, ALU.add, reads=[r1, r2], writes=[dres])
        self.cp("act", dst[32:64, :], self.pst[k][32:64, :], reads=[self.psn(k)], writes=[dres])
        self.cp("act", dst[64:128, :], self.pst[k][64:128, :], reads=[self.psn(k)], writes=[dres])

    def group_nsa(self, L):
        w = self.w_in[L]
        cst = self.cst
        qT = self.carve([128, 4, S], BF16)
        kT = self.carve([128, 3, S], BF16)
        vcT = self.carve([128, S], BF16)
        v1 = self.carve([128, 2, NT, 130], BF16)
        gsig = self.carve([128, NT, 12], F32)
        kcT = self.carve([128, 128], BF16)
        vc = self.carve([128, 128], BF16)
        w1b = self.carve([128, 2, 32, 128], BF16)
        w1 = self.w_cmp1[L]
        for m in range(2):
            for q4 in range(4):
                src = w1[m, q4 * 1024:(q4 + 1) * 1024, :].rearrange("(a p) j -> p a j", p=128)
                self.dma(w1b[:, m, q4 * 8:(q4 + 1) * 8, :], src, reads=[], writes=["w1b"],
                         slot=("w1cast", m, q4), eng="pool")
        mark = self.aoff
        self.rope_setup()
        self.memset("pool", v1[:, :, :, 128:129], 1.0, writes=["v1"])
        slot = self.take_pref(lambda: self.wload(w[:, C_NSA_Q:C_NSA_Q + 512], 512))
        kk = 0
        for g in range(4):
            for h in range(4):
                k = kk % 6
                kk += 1
                self.proj_fm(slot, h * 128, g, k)
                self.rope_evac(k, qT[:, h, g * 512:(g + 1) * 512], g, "qT")
        slot = self.wload(w[:, C_NSA_KV:C_NSA_KV + 512], 512)
        for br in range(3):
            for g in range(4):
                k = kk % 6
                kk += 1
                self.proj_fm(slot, br * 128, g, k)
                self.rope_evac(k, kT[:, br, g * 512:(g + 1) * 512], g, "kT")
        for g in range(4):
            k = kk % 6
            kk += 1
            self.proj_fm(slot, 384, g, k)
            self.cp("act", vcT[:, g * 512:(g + 1) * 512], self.pst[k][:, :],
                    reads=[self.psn(k)], writes=["vcT"])
        slot = self.wload(w[:, C_NSA_KV + 512:C_NSA_KV + 512 + 268], 268)
        for t in range(NT):
            k = kk % 6
            kk += 1
            self.proj_tm(slot, 0, 268, t, k)
            self.cp("dve", v1[:, :, t, 0:128],
                    self.pst[k][:, 0:256].rearrange("p (b d) -> p b d", b=2),
                    reads=[self.psn(k)], writes=["v1"])
            self.act(gsig[:, t, :], self.pst[k][:, 256:268], AF.Sigmoid,
                     reads=[self.psn(k)], writes=["gsig"])
        zslot = self.wload(w[:, C_NSA_Z:C_NSA_Z + 512], 512)
        assert self.aoff <= self.p0_top, (self.aoff, self.p0_top)
        self.P.barrier()
        self.aoff = mark
        self.run_pref()

        w1 = self.w_cmp1[L]
        w2 = self.w_cmp2[L]
        pe = self.w_pe[L]
        w2b = self.carve([128, 2, 128], BF16)
        w2f = self.carve([128, 2, 128], F32)
        pef = self.carve([32, 2, 128], F32)
        peT = self.carve([128, 2, 32], BF16)
        cvec = self.carve([128, 2], F32)
        hid = self.carve([128, 2, 128], BF16)
        self.dma(w2f, w2.rearrange("m j d -> j m d"), reads=[], writes=["w2f"], slot="w2f")
        self.cp("pool", w2b, w2f, reads=["w2f"], writes=["w2b"])
        self.dma(pef, pe.rearrange("m p d -> p m d"), reads=[], writes=["pef"], slot="pef")
        for m in range(2):
            self.tr(self.pst[6][:, m * 32:(m + 1) * 32], pef[:, m, :], self.identf[0:32, 0:32],
                    reads=["pef", "identf"], writes=[self.psn(6)])
        self.cp("dve", peT, self.pst[6][:, 0:64].rearrange("p (m a) -> p m a", m=2),
                reads=[self.psn(6)], writes=["peT"])
        self.memset("pool", kcT, 0.0, writes=["kcT"])
        self.memset("pool", vc, 0.0, writes=["vc"])
        self.memset("pool", hid, 0.0, writes=["hid"])
        srcT = [kT[:, 0, :], vcT]
        for m in range(2):
            for p in range(32):
                self.mm(self.pst[6][:, 64 + m:65 + m], w1b[:, m, p, :], peT[:, m, p:p + 1], p == 0, p == 31,
                        reads=["w1b", "peT"], writes=[self.psn(6)])
            self.cp("dve", cvec[:, m:m + 1], self.pst[6][:, 64 + m:65 + m], reads=[self.psn(6)], writes=["cvec"])
            for p in range(32):
                self.mm(self.pst[m][:, 0:127], w1b[:, m, p, :], srcT[m][:, p:p + 16 * 126 + 1:16],
                        p == 0, p == 31, reads=["w1b", "kT", "vcT"], writes=[self.psn(m)])
            self.act(hid[:, m, 0:127], self.pst[m][:, 0:127], AF.Silu, bias=cvec[:, m:m + 1],
                     reads=[self.psn(m), "cvec"], writes=["hid"])
        self.mm(self.pst[2][:, 0:127], w2b[:, 0, :], hid[:, 0, 0:127], True, True,
                reads=["w2b", "hid"], writes=[self.psn(2)])
        self.cp("dve", kcT[:, 0:127], self.pst[2][:, 0:127], reads=[self.psn(2)], writes=["kcT"])
        self.mm(self.pst[3][0:127, 0:128], hid[:, 1, 0:127], w2b[:, 1, :], True, True,
                reads=["w2b", "hid"], writes=[self.psn(3)])
        self.cp("dve", vc[0:127, :], self.pst[3][0:127, 0:128], reads=[self.psn(3)], writes=["vc"])

        cmpb = self.load_const("cmpb", cst["cmpb"], [128, S], BF16)
        esel = self.load_const("esel", cst["esel"], [32, NT, 128], BF16)
        cbd = self.load_const("cbd", cst["cbd"], [128, 128], BF16)
        cbw = self.load_const("cbw", cst["cbw"], [128, 128], BF16)
        ovl = self.load_const("ovl", cst["ovl"], [128, 34], BF16)
        m1 = self.load_const("m1", cst["m1"], [128, NT, 32], F32)
        m2 = self.load_const("m2", cst["m2"], [128, NT, 32], F32)
        selT = [self.carve([32, 128], BF16) for _ in range(2)]
        eb = [self.carve([128, 512], BF16) for _ in range(3)]
        U = self.carve([128, 4, 34], F32)
        t32 = self.carve([128, 4, 32], F32)
        imp = self.carve([128, 32], F32)
        top8 = self.carve([128, 8], F32)
        selm = self.carve([128, 32], BF16)
        Lall = self.carve([128, 12], F32)
        coef = self.carve([128, 12], F32)
        acc = [self.carve([128, 512], F32) for _ in range(2)]
        sz = [self.carve([128, 512], F32) for _ in range(2)]
        yo = [self.carve([128, 512], BF16) for _ in range(2)]
        ei = 0
        ocn = [self.carve([128, 512], F32) for _ in range(2)]
        rLc = self.carve([128, 4], F32)
        self.memset("pool", Lall[:, 0:4], 1.0, writes=["Lall"])
        eic = [0]

        def cmp_stage(t):
            qs = slice(t * 128, (t + 1) * 128)
            qv = qT[:, :, qs]
            sT = selT[t % 2]
            sTr = ("selT", t % 2)
            kS = eic[0] % 2
            e = eb[eic[0] % 3]
            er = ("eb", eic[0] % 3)
            eic[0] += 1
            sv = self.pst[kS][:, :].rearrange("p (h q) -> p h q", h=4)
            self.mm(sv, kcT, qv, True, False, reads=["kcT", "qT"], writes=[self.psn(kS)])
            self.mm(sv, self.identb, cmpb[:, qs].unsqueeze(1).broadcast_to([128, 4, 128]), False, True,
                    reads=["cmpb", "identb"], writes=[self.psn(kS)])
            self.act(e, self.pst[kS][:, :], AF.Exp, scale=SCALE, reads=[self.psn(kS)], writes=[er])
            for h in range(4):
                self.mm(self.pst[6][:, h * 128:(h + 1) * 128], e[:, h * 128:(h + 1) * 128], vc, h == 0, h == 3,
                        reads=[er, "vc"], writes=[self.psn(6)])
            for h in range(4):
                self.mm(self.pst[7][:, h * 34:(h + 1) * 34], e[:, h * 128:(h + 1) * 128], ovl, h == 0, h == 3,
                        reads=[er, "ovl"], writes=[self.psn(7)])
            self.cp("dve", U, self.pst[7][:, 0:136].rearrange("p (h j) -> p h j", h=4),
                    reads=[self.psn(7)], writes=["U"])
            self.ts("dve", rLc, U[:, :, 32], 1e-30, ALU.max, reads=["U"], writes=["rLc"])
            self.P.op("dve", lambda e_, o=rLc, i=rLc: e_.reciprocal(out=o, in_=i),
                      reads=["rLc"], writes=["rLc"])
            self.tt("dve", ocn[t % 2].rearrange("p (h d) -> p h d", h=4),
                    self.pst[6][:, :].rearrange("p (h d) -> p h d", h=4),
                    rLc.unsqueeze(2).broadcast_to([128, 4, 128]), ALU.mult,
                    reads=[self.psn(6), "rLc"], writes=[("ocn", t % 2)])
            self.tt("dve", t32, U[:, :, 0:32], rLc.unsqueeze(2).broadcast_to([128, 4, 32]), ALU.mult,
                    reads=["U", "rLc"], writes=["t32"])
            self.P.op("dve", lambda e_, o=imp, i=t32.rearrange("p h j -> p j h"): e_.tensor_reduce(
                out=o, in_=i, axis=AX.X, op=ALU.add), reads=["t32"], writes=["imp"])
            self.tt("dve", imp, imp, m1[:, t, :], ALU.mult, reads=["imp", "m1"], writes=["imp"])
            self.tt("dve", imp, imp, m2[:, t, :], ALU.add, reads=["imp", "m2"], writes=["imp"])
            self.P.op("dve", lambda e_, o=top8, i=imp: e_.max(out=o, in_=i), reads=["imp"], writes=["top8"])
            self.ts("dve", selm, imp, top8[:, 7:8], ALU.is_ge, reads=["imp", "top8"], writes=["selm"],
                    s2=-1.0, op1=ALU.add)
            pb = self.pst[7][:, :].bitcast(BF16)
            self.tr(pb[0:32, 0:128], selm, self.identb, reads=["selm", "identb"], writes=[self.psn(7)])
            self.cp("act", sT, pb[0:32, 0:128], reads=[self.psn(7)], writes=[sTr])

        cmp_stage(0)
        for t in range(NT):
            qs = slice(t * 128, (t + 1) * 128)
            qv = qT[:, :, qs]
            sT = selT[t % 2]
            sTr = ("selT", t % 2)
            if t + 1 < NT:
                cmp_stage(t + 1)
            self.proj_tm(zslot, 0, 512, t, 7)
            self.act(sz[t % 2], self.pst[7][:, :], AF.Silu, reads=[self.psn(7)], writes=[("sz", t % 2)])
            for br, kts in ((2, [kt for kt in (t - 2, t - 1, t) if kt >= 0]), (1, list(range(0, t + 1)))):
                ob = (2, 3) if br == 1 else (4, 5)
                for ii, kt in enumerate(kts):
                    kS = eic[0] % 2
                    e = eb[eic[0] % 3]
                    er = ("eb", eic[0] % 3)
                    eic[0] += 1
                    ks = slice(kt * 128, (kt + 1) * 128)
                    sv = self.pst[kS][:, :].rearrange("p (h q) -> p h q", h=4)
                    extra = []
                    if br == 1:
                        extra.append((esel[:, kt, :], sT.unsqueeze(1).broadcast_to([32, 4, 128]),
                                      ["esel", sTr]))
                    if kt == t:
                        extra.append((self.identb, cbd.unsqueeze(1).broadcast_to([128, 4, 128]),
                                      ["cbd", "identb"]))
                    elif br == 2 and kt == t - 2:
                        extra.append((self.identb, cbw.unsqueeze(1).broadcast_to([128, 4, 128]),
                                      ["cbw", "identb"]))
                    self.mm(sv, kT[:, br, ks], qv, True, len(extra) == 0, reads=["kT", "qT"], writes=[self.psn(kS)])
                    for xi, (l_, r_, rd) in enumerate(extra):
                        self.mm(sv, l_, r_, False, xi == len(extra) - 1, reads=rd, writes=[self.psn(kS)])
                    self.act(e, self.pst[kS][:, :], AF.Exp, scale=SCALE, reads=[self.psn(kS)], writes=[er])
                    for h in range(4):
                        kb = ob[h // 2]
                        c0 = (h % 2) * 130
                        self.mm(self.pst[kb][:, c0:c0 + 129], e[:, h * 128:(h + 1) * 128], v1[:, br - 1, kt, 0:129],
                                (ii == 0 and h % 2 == 0), (ii == len(kts) - 1),
                                reads=[er, "v1"], writes=[self.psn(kb)])
            for bi, ob in enumerate(((2, 3), (4, 5))):
                for hh in range(2):
                    self.cp("dve", Lall[:, 4 + bi * 4 + hh * 2:4 + bi * 4 + hh * 2 + 2],
                            self.pst[ob[hh]][:, 0:260].rearrange("p (h c) -> p h c", h=2)[:, :, 128],
                            reads=[self.psn(ob[hh])], writes=["Lall"])
            self.ts("dve", Lall[:, 4:12], Lall[:, 4:12], 1e-30, ALU.max, reads=["Lall"], writes=["Lall"])
            self.P.op("dve", lambda e_, o=Lall[:, 4:12], i=Lall[:, 4:12]: e_.reciprocal(out=o, in_=i),
                      reads=["Lall"], writes=["Lall"])
            gv = gsig[:, t, :].rearrange("p (h b) -> p b h", b=3)
            self.tt("dve", coef.rearrange("p (b h) -> p b h", b=3), gv, Lall.rearrange("p (b h) -> p b h", b=3),
                    ALU.mult, reads=["gsig", "Lall"], writes=["coef"])
            a_ = acc[t % 2]
            ar = ("acc", t % 2)
            for h in range(4):
                hs = slice(h * 128, (h + 1) * 128)
                c0 = (h % 2) * 130
                self.ts("dve", a_[:, hs], ocn[t % 2][:, hs], coef[:, h:h + 1], ALU.mult,
                        reads=[("ocn", t % 2), "coef"], writes=[ar])
                for bi, ob in enumerate(((2, 3), (4, 5))):
                    self.stt(a_[:, hs], self.pst[ob[h // 2]][:, c0:c0 + 128],
                             coef[:, 4 * (bi + 1) + h:4 * (bi + 1) + h + 1],
                             a_[:, hs], ALU.mult, ALU.add, reads=[self.psn(ob[h // 2]), "coef", ar], writes=[ar])
            if self.dbg:
                self.dma(self.dbg_ya[qs, :], a_, reads=[ar], writes=[], slot="dbgya")
            self.tt("pool", yo[t % 2], a_, sz[t % 2], ALU.mult, reads=[ar, ("sz", t % 2)], writes=[("yo", t % 2)])
            self.dma(self.y_tm[qs, 0:512], yo[t % 2], reads=[("yo", t % 2)], writes=["y_tm"], slot=("yo", t % 2))
        self.phase_end()


    def group_moba(self, L):
        w = self.w_in[L]
        cst = self.cst
        qT = self.carve([128, 4, S], BF16)
        kT = self.carve([128, 4, S], BF16)
        v1 = self.carve([128, 4, NT, 130], BF16)
        szall = self.carve([128, NT, 512], BF16)
        mark = self.aoff
        self.rope_setup()
        self.memset("pool", v1[:, :, :, 128:129], 1.0, writes=["v1"])
        kk = 0
        for dstT, c0 in ((qT, C_MB_QKV), (kT, C_MB_QKV + 512)):
            if dstT is qT:
                slot = self.take_pref(lambda: self.wload(w[:, c0:c0 + 512], 512))
            else:
                slot = self.wload(w[:, c0:c0 + 512], 512)
            for h in range(4):
                for g in range(4):
                    k = kk % 6
                    kk += 1
                    self.proj_fm(slot, h * 128, g, k)
                    self.rope_evac(k, dstT[:, h, g * 512:(g + 1) * 512], g, "qT" if dstT is qT else "kT")
        slot = self.wload(w[:, C_MB_QKV + 1024:C_MB_QKV + 1536], 512)
        for t in range(NT):
            k = kk % 6
            kk += 1
            self.proj_tm(slot, 0, 512, t, k)
            self.cp("dve" if t % 2 else "act", v1[:, :, t, 0:128],
                    self.pst[k][:, :].rearrange("p (h d) -> p h d", h=4), reads=[self.psn(k)], writes=["v1"])
        zslot = self.wload(w[:, C_MB_Z:C_MB_Z + 512], 512)
        for t in range(NT):
            k = kk % 6
            kk += 1
            self.proj_tm(zslot, 0, 512, t, k)
            self.act(szall[:, t, :], self.pst[k][:, :], AF.Silu, reads=[self.psn(k)], writes=["szall"])
        self.run_pref()
        kmf = self.carve([128, 32], F32)
        kmb = self.carve([128, 4, 8], BF16)
        self.P.op("dve", lambda e_, o=kmf, i=kT.rearrange("p h (b k) -> p (h b) k", b=8): e_.tensor_reduce(
            out=o, in_=i, axis=AX.X, op=ALU.add), reads=["kT"], writes=["kmf"], cost=8.6)
        self.ts("dve", kmb.rearrange("p h b -> p (h b)"), kmf, 1.0 / 256.0, ALU.mult, reads=["kmf"], writes=["kmb"])
        e8 = self.load_const("e8", cst["e8"], [8, NT, 128], BF16)
        cbd = self.load_const("cbd", cst["cbd"], [128, 128], BF16)
        pastb = self.load_const("pastb", cst["pastb"], [128, NT, 8], F32)
        pastm = self.load_const("pastm", cst["pastm"], [128, NT, 8], F32)
        gs2 = self.carve([128, 4, 8], F32)
        top8 = self.carve([128, 4, 8], F32)
        selm = self.carve([128, 4, 8], F32)
        selmb = self.carve([128, 4, 8], BF16)
        sTs = [self.carve([8, 4, 128], BF16) for _ in range(2)]
        eb = [self.carve([128, 512], BF16) for _ in range(3)]
        Lall = self.carve([128, 4], F32)
        acc = [self.carve([128, 512], F32) for _ in range(2)]
        yo = [self.carve([128, 512], BF16) for _ in range(2)]
        ei = 0
        for t in range(NT):
            qs = slice(t * 128, (t + 1) * 128)
            cur = t // 2
            sT = sTs[t % 2]
            sTr = ("sT", t % 2)
            ob = (2, 3) if t % 2 == 0 else (4, 5)
            if cur > 0:
                for h in range(4):
                    self.mm(self.pst[7][:, h * 8:(h + 1) * 8], qT[:, h, qs], kmb[:, h, :], h == 0, h == 3,
                            reads=["qT", "kmb"], writes=[self.psn(7)])
                self.tt("dve", gs2, self.pst[7][:, 0:32].rearrange("p (h b) -> p h b", h=4),
                        pastb[:, t, :].unsqueeze(1).broadcast_to([128, 4, 8]), ALU.add,
                        reads=[self.psn(7), "pastb"], writes=["gs2"])
                for h in range(4):
                    self.P.op("dve", lambda e_, o=top8[:, h, :], i=gs2[:, h, :]: e_.max(out=o, in_=i),
                              reads=["gs2"], writes=["top8"])
                for h in range(4):
                    self.ts("dve", selm[:, h, :], gs2[:, h, :], top8[:, h, 2:3], ALU.is_ge,
                            reads=["gs2", "top8"], writes=["selm"])
                self.tt("dve", selm, selm, pastm[:, t, :].unsqueeze(1).broadcast_to([128, 4, 8]), ALU.mult,
                        reads=["selm", "pastm"], writes=["selm"])
                self.ts("dve", selmb, selm, -1.0, ALU.add, reads=["selm"], writes=["selmb"])
                pb = self.pst[7][:, :].bitcast(BF16)
                for h in range(4):
                    self.tr(pb[0:8, h * 128:(h + 1) * 128], selmb[:, h, :], self.identb,
                            reads=["selmb", "identb"], writes=[self.psn(7)])
                self.cp("act", sT, pb[0:8, 0:512].rearrange("p (h q) -> p h q", h=4),
                        reads=[self.psn(7)], writes=[sTr])
            kts = list(range(0, t + 1))
            for ii, kt in enumerate(kts):
                kS = ei % 2
                e = eb[ei % 3]
                er = ("eb", ei % 3)
                ei += 1
                ks = slice(kt * 128, (kt + 1) * 128)
                extra = []
                if kt < 2 * cur:
                    extra.append((e8[:, kt, :], sT.rearrange("p h q -> p (h q)"), ["e8", sTr]))
                if kt == t:
                    extra.append((self.identb, cbd.unsqueeze(1).broadcast_to([128, 4, 128]), ["cbd", "identb"]))
                for h in range(4):
                    self.mm(self.pst[kS][:, h * 128:(h + 1) * 128], kT[:, h, ks], qT[:, h, qs], h == 0,
                            (h == 3 and len(extra) == 0), reads=["kT", "qT"], writes=[self.psn(kS)])
                for xi, (l_, r_, rd) in enumerate(extra):
                    o_ = self.pst[kS][:, :]
                    if len(r_.shape) == 3:
                        o_ = o_.rearrange("p (h q) -> p h q", h=4)
                    self.mm(o_, l_, r_, False, xi == len(extra) - 1, reads=rd, writes=[self.psn(kS)])
                self.act(e, self.pst[kS][:, :], AF.Exp, scale=SCALE, reads=[self.psn(kS)], writes=[er])
                for h in range(4):
                    kb = ob[h // 2]
                    c0 = (h % 2) * 130
                    self.mm(self.pst[kb][:, c0:c0 + 129], e[:, h * 128:(h + 1) * 128], v1[:, h, kt, 0:129],
                            (ii == 0 and h % 2 == 0), (ii == len(kts) - 1),
                            reads=[er, "v1"], writes=[self.psn(kb)])
            for hh in range(2):
                self.cp("dve", Lall[:, hh * 2:hh * 2 + 2],
                        self.pst[ob[hh]][:, 0:260].rearrange("p (h c) -> p h c", h=2)[:, :, 128],
                        reads=[self.psn(ob[hh])], writes=["Lall"])
            self.ts("dve", Lall, Lall, 1e-30, ALU.max, reads=["Lall"], writes=["Lall"])
            self.P.op("dve", lambda e_, o=Lall, i=Lall: e_.reciprocal(out=o, in_=i), reads=["Lall"], writes=["Lall"])
            a_ = acc[t % 2]
            ar = ("acc", t % 2)
            for h in range(4):
                hs = slice(h * 128, (h + 1) * 128)
                c0 = (h % 2) * 130
                self.ts("dve", a_[:, hs], self.pst[ob[h // 2]][:, c0:c0 + 128], Lall[:, h:h + 1], ALU.mult,
                        reads=[self.psn(ob[h // 2]), "Lall"], writes=[ar])
            self.tt("pool", yo[t % 2], a_, szall[:, t, :], ALU.mult, reads=[ar, "szall"], writes=[("yo", t % 2)])
            self.dma(self.y_tm[qs, 1536:2048], yo[t % 2], reads=[("yo", t % 2)], writes=["y_tm"], slot=("yo", t % 2))
        self.phase_end()

    def group_mlstm(self, L):
        w = self.w_in[L]
        cst = self.cst
        prm = self.prm
        qT = self.carve([128, 4, S], BF16)
        kT = self.carve([128, 4, S], BF16)
        v1 = self.carve([128, 4, NT, 130], BF16)
        ifp = self.carve([128, NT, 8], F32)
        self.memset("pool", v1[:, :, :, 128:129], 1.0, writes=["v1"])
        kk = 0
        for dstT, c0, nm in ((qT, C_ML_QKV, "qT"), (kT, C_ML_QKV + 512, "kT")):
            if nm == "qT":
                slot = self.take_pref(lambda: self.wload(w[:, c0:c0 + 512], 512))
            else:
                slot = self.wload(w[:, c0:c0 + 512], 512)
            for h in range(4):
                for g in range(4):
                    k = kk % 6
                    kk += 1
                    self.proj_fm(slot, h * 128, g, k)
                    self.cp("act" if kk % 2 else "dve", dstT[:, h, g * 512:(g + 1) * 512], self.pst[k][:, :],
                            reads=[self.psn(k)], writes=[nm])
        slot = self.wload(w[:, C_ML_QKV + 1024:C_ML_QKV + 1536], 512)
        for t in range(NT):
            k = kk % 6
            kk += 1
            self.proj_tm(slot, 0, 512, t, k)
            self.cp("dve" if t % 2 else "act", v1[:, :, t, 0:128],
                    self.pst[k][:, :].rearrange("p (h d) -> p h d", h=4), reads=[self.psn(k)], writes=["v1"])
        slot = self.wload(w[:, C_ML_IF:C_ML_IF + 8], 8)
        for t in range(NT):
            for dc in range(16):
                self.mm(self.pst[6][:, t * 8:(t + 1) * 8], self.xT[:, dc, t * 128:(t + 1) * 128],
                        self.wb[slot][:, dc, 0:8], dc == 0, dc == 15,
                        reads=[("wb", slot), ("xT", t)], writes=[self.psn(6)])
        self.cp("dve", ifp, self.pst[6][:, 0:128].rearrange("p (t g) -> p t g", g=8), reads=[self.psn(6)], writes=["ifp"])
        oslot = self.wload(w[:, C_ML_O:C_ML_O + 512], 512)
        zslot = self.wload(w[:, C_ML_Z:C_ML_Z + 512], 512)
        bias8 = self.carve([128, 8], F32)
        self.dma(bias8[:, 0:4], prm["mlstm_i_bias"][L:L + 1, :].broadcast_to([128, 4]), reads=[], writes=["bias8"], slot="bias8a")
        self.dma(bias8[:, 4:8], prm["mlstm_f_bias"][L:L + 1, :].broadcast_to([128, 4]), reads=[], writes=["bias8"], slot="bias8b")
        ng = self.carve([128, 512], F32)
        self.dma(ng, prm["mlstm_norm_g"][L:L + 1, :].broadcast_to([128, 512]), reads=[], writes=["ng"], slot="ng")
        triu = self.load_const("triu", cst["triu"], [128, 128], F32)
        onesf = self.load_const("onesf", cst["onesf"], [128, 128], F32)
        caus = self.load_const("caus01", cst["caus01"], [128, 128], BF16)
        self.tt("dve", ifp, ifp, bias8.unsqueeze(1).broadcast_to([128, NT, 8]), ALU.add, reads=["ifp", "bias8"], writes=["ifp"])
        nl = self.carve([128, NT, 4], F32)
        nbS = self.carve([128, 2, NT, 4], F32)
        d1 = self.carve([128, NT, 4], F32)
        fack = self.carve([128, NT, 4], F32)
        rowf = self.carve([128, NT, 4], F32)
        edec = self.carve([128, NT, 4], F32)
        self.act(nl, ifp[:, :, 4:8], AF.Exp, scale=-1.0, reads=["ifp"], writes=["nl"])
        self.act(nl, nl, AF.Ln, bias=1.0, reads=["nl"], writes=["nl"])
        nlf = nl.rearrange("p t h -> p (t h)")
        self.mm(self.pst[6][:, 0:64], triu, nlf, True, True, reads=["triu", "nl"], writes=[self.psn(6)])
        self.mm(self.pst[6][:, 64:128], onesf, nlf, True, True, reads=["onesf", "nl"], writes=[self.psn(6)])
        self.cp("dve", nbS.rearrange("p a t h -> p (a t h)"), self.pst[6][:, 0:128], reads=[self.psn(6)], writes=["nbS"])
        self.tt("dve", d1, nbS[:, 0], nbS[:, 1], ALU.subtract, reads=["nbS"], writes=["d1"])
        self.tt("dve", fack, ifp[:, :, 0:4], d1, ALU.add, reads=["ifp", "d1"], writes=["fack"])
        self.act(fack, fack, AF.Exp, bias=float(np.log(SCALE)), reads=["fack"], writes=["fack"])
        self.act(rowf, d1, AF.Exp, scale=-1.0, reads=["d1"], writes=["rowf"])
        self.act(edec, nbS[:, 1], AF.Exp, scale=-1.0, reads=["nbS"], writes=["edec"])
        C = self.carve([128, 4, 130], F32)
        Cd = self.carve([128, 4, 130], F32)
        Cb = self.carve([128, 4, 130], BF16)
        Gm = [self.carve([128, 512], BF16) for _ in range(2)]
        ktm = [self.carve([128, 512], BF16) for _ in range(2)]
        Vh = [self.carve([128, 4, 130], BF16) for _ in range(2)]
        den = self.carve([128, 4], F32)
        rr = self.carve([128, 4], F32)
        so = [self.carve([128, 512], F32) for _ in range(2)]
        sz = [self.carve([128, 512], F32) for _ in range(2)]
        hg = [self.carve([128, 512], F32) for _ in range(2)]
        st6 = self.carve([128, 4, 6], F32)
        mv = self.carve([128, 4, 2], F32)
        rstd = self.carve([128, 4], F32)
        yo = [self.carve([128, 512], BF16) for _ in range(2)]
        for c in range(NT):
            qs = slice(c * 128, (c + 1) * 128)
            i2 = c % 2
            gS = c % 2
            for h in range(4):
                self.mm(self.pst[gS][:, h * 128:(h + 1) * 128], kT[:, h, qs], qT[:, h, qs], h == 0, h == 3,
                        reads=["kT", "qT"], writes=[self.psn(gS)])
            for h in range(4):
                hs = slice(h * 128, (h + 1) * 128)
                self.stt(Gm[i2][:, hs], self.pst[gS][:, hs], fack[:, c, h:h + 1], caus, ALU.mult, ALU.mult,
                         reads=[self.psn(gS), "fack", "caus01"], writes=[("Gm", i2)])
            pb = self.pst[2][:, :].bitcast(BF16)
            for h in range(4):
                self.tr(pb[:, h * 128:(h + 1) * 128], kT[:, h, qs], self.identb, reads=["kT", "identb"], writes=[self.psn(2)])
            self.cp("act", ktm[i2], pb[:, 0:512], reads=[self.psn(2)], writes=[("ktm", i2)])
            self.tt("pool", Vh[i2][:, :, 0:129], v1[:, :, c, 0:129],
                    fack[:, c, :].unsqueeze(2).broadcast_to([128, 4, 129]), ALU.mult,
                    reads=["v1", "fack"], writes=[("Vh", i2)])
            if c > 0:
                for h in range(4):
                    self.ts("dve", Cd[:, h, 0:129], C[:, h, 0:129], edec[:, c, h:h + 1], ALU.mult,
                            reads=["C", "edec"], writes=["Cd"])
                self.cp("pool", Cb[:, :, 0:129], Cd[:, :, 0:129], reads=["Cd"], writes=["Cb"])
            ob = (3, 4)
            for h in range(4):
                kb = ob[h // 2]
                c0 = (h % 2) * 130
                self.mm(self.pst[kb][:, c0:c0 + 129], Gm[i2][:, h * 128:(h + 1) * 128], v1[:, h, c, 0:129],
                        h % 2 == 0, c == 0, reads=[("Gm", i2), "v1"], writes=[self.psn(kb)])
                if c > 0:
                    self.mm(self.pst[kb][:, c0:c0 + 129], qT[:, h, qs], Cb[:, h, 0:129], False, True,
                            reads=["qT", "Cb"], writes=[self.psn(kb)])
            sbk = (5, 6)
            for h in range(4):
                kb = sbk[h // 2]
                c0 = (h % 2) * 130
                self.mm(self.pst[kb][:, c0:c0 + 129], ktm[i2][:, h * 128:(h + 1) * 128], Vh[i2][:, h, 0:129],
                        h % 2 == 0, True, reads=[("ktm", i2), ("Vh", i2)], writes=[self.psn(kb)])
            for j in range(2):
                pv = self.pst[sbk[j]][:, 0:260].rearrange("p (h c) -> p h c", h=2)[:, :, 0:129]
                if c == 0:
                    self.cp("dve", C[:, 2 * j:2 * j + 2, 0:129], pv, reads=[self.psn(sbk[j])], writes=["C"])
                else:
                    self.tt("dve", C[:, 2 * j:2 * j + 2, 0:129], Cd[:, 2 * j:2 * j + 2, 0:129], pv, ALU.add,
                            reads=[self.psn(sbk[j]), "Cd"], writes=["C"])
            for j in range(2):
                self.cp("dve", den[:, 2 * j:2 * j + 2],
                        self.pst[ob[j]][:, 0:260].rearrange("p (h c) -> p h c", h=2)[:, :, 128],
                        reads=[self.psn(ob[j])], writes=["den"])
            self.tt("dve", den, den, rowf[:, c, :], ALU.mult, reads=["den", "rowf"], writes=["den"])
            self.stt(den, den, -1.0, den, ALU.mult, ALU.max, reads=["den"], writes=["den"])
            self.ts("dve", den, den, 1.0, ALU.max, reads=["den"], writes=["den"])
            self.P.op("dve", lambda e_, o=den, i=den: e_.reciprocal(out=o, in_=i), reads=["den"], writes=["den"])
            self.tt("dve", rr, den, rowf[:, c, :], ALU.mult, reads=["den", "rowf"], writes=["rr"])
            self.proj_tm(oslot, 0, 512, c, 7)
            self.act(so[i2], self.pst[7][:, :], AF.Sigmoid, reads=[self.psn(7)], writes=[("so", i2)])
            self.proj_tm(zslot, 0, 512, c, 7)
            self.act(sz[i2], self.pst[7][:, :], AF.Silu, reads=[self.psn(7)], writes=[("sz", i2)])
            for h in range(4):
                hs = slice(h * 128, (h + 1) * 128)
                c0 = (h % 2) * 130
                self.stt(hg[i2][:, hs], self.pst[ob[h // 2]][:, c0:c0 + 128], rr[:, h:h + 1], so[i2][:, hs],
                         ALU.mult, ALU.mult, reads=[self.psn(ob[h // 2]), "rr", ("so", i2)], writes=[("hg", i2)])
            for h in range(4):
                hs = slice(h * 128, (h + 1) * 128)
                self.P.op("dve", lambda e_, o=st6[:, h, :], i=hg[i2][:, hs]: e_.bn_stats(out=o, in_=i),
                          reads=[("hg", i2)], writes=["st6"])
            for h in range(4):
                self.P.op("dve", lambda e_, o=mv[:, h, :], i=st6[:, h, :]: e_.bn_aggr(out=o, in_=i),
                          reads=["st6"], writes=["mv"])
            self.ts("dve", rstd, mv[:, :, 1], 1e-5, ALU.add, reads=["mv"], writes=["rstd"])
            self.act(rstd, rstd, AF.Ln, reads=["rstd"], writes=["rstd"])
            self.act(rstd, rstd, AF.Exp, scale=-0.5, reads=["rstd"], writes=["rstd"])
            for h in range(4):
                hs = slice(h * 128, (h + 1) * 128)
                self.ts("dve", hg[i2][:, hs], hg[i2][:, hs], mv[:, h, 0:1], ALU.subtract,
                        reads=[("hg", i2), "mv", "rstd"], writes=[("hg", i2)], s2=rstd[:, h:h + 1], op1=ALU.mult)
            self.tt("pool", hg[i2], hg[i2], ng, ALU.mult, reads=[("hg", i2), "ng"], writes=[("hg", i2)])
            self.tt("pool", yo[i2], hg[i2], sz[i2], ALU.mult, reads=[("hg", i2), ("sz", i2)], writes=[("yo", i2)])
            self.dma(self.y_tm[qs, 512:1024], yo[i2], reads=[("yo", i2)], writes=["y_tm"], slot=("yo", i2))
        self.run_pref()
        self.phase_end()

    def group_lru(self, L):
        w = self.w_in[L]
        prm = self.prm
        cw = self.carve([128, 4, 4], F32)
        cb = self.carve([128, 4], F32)
        gb = self.carve([128, 2, 4], F32)
        lam = self.carve([128, 4], F32)
        clam = self.carve([128, 4], F32)
        gwf = self.carve([128, 8, 128], F32)
        gwb = self.carve([128, 8, 128], BF16)
        for n in range(4):
            self.dma(cw[:, n, :], prm["lru_conv_w"][L][:, n * 128:(n + 1) * 128].rearrange("j c -> c j"),
                     reads=[], writes=["cw"], slot="cw", nc_ok=True)
        self.dma(cb, prm["lru_conv_b"][L].rearrange("(n c) -> c n", c=128), reads=[], writes=["cb"], slot="cb", nc_ok=True)
        for g_ in range(2):
            self.dma(gb[:, g_, :], prm["lru_gate_b"][L][g_].rearrange("(n c) -> c n", c=128),
                     reads=[], writes=["gb"], slot="gb", nc_ok=True)
        self.dma(lam, prm["lru_lambda"][L].rearrange("(n c) -> c n", c=128), reads=[], writes=["lam"], slot="lam", nc_ok=True)
        self.dma(gwf, prm["lru_gate_w"][L].rearrange("g n d e -> d (g n) e"), reads=[], writes=["gwf"], slot="gwf")
        self.cp("pool", gwb, gwf, reads=["gwf"], writes=["gwb"])
        self.act(clam, lam, AF.Exp, scale=-1.0, reads=["lam"], writes=["clam"])
        self.act(clam, clam, AF.Ln, bias=1.0, reads=["clam"], writes=["clam"])
        self.ts("dve", clam, clam, -8.0, ALU.mult, reads=["clam"], writes=["clam"])
        xslot = self.take_pref(lambda: self.wload(w[:, C_LRU_X:C_LRU_X + 512], 512))
        zslot = self.wload(w[:, C_LRU_Z:C_LRU_Z + 512], 512)
        xr = self.carve([128, S], F32)
        u = self.carve([128, S], F32)
        ub = self.carve([128, S], BF16)
        rr = self.carve([128, S], F32)
        ii_ = self.carve([128, S], F32)
        aa = self.carve([128, S], F32)
        szT = self.carve([128, S], BF16)
        yo = self.carve([128, S], BF16)
        kk = 0
        for n in range(4):
            for g in range(4):
                k = kk % 8
                kk += 1
                self.proj_fm(xslot, n * 128, g, k)
                self.cp("act" if g % 2 else "dve", xr[:, g * 512:(g + 1) * 512], self.pst[k][:, :],
                        reads=[self.psn(k)], writes=["xr"])
            for g in range(4):
                k = kk % 8
                kk += 1
                self.proj_fm(zslot, n * 128, g, k)
                self.act(szT[:, g * 512:(g + 1) * 512], self.pst[k][:, :], AF.Silu, reads=[self.psn(k)], writes=["szT"])
            self.ts("dve", u, xr, cw[:, n, 3:4], ALU.mult, reads=["xr", "cw", "cb"], writes=["u"],
                    s2=cb[:, n:n + 1], op1=ALU.add)
            for j in range(1, 4):
                self.stt(u[:, j:S], xr[:, 0:S - j], cw[:, n, 3 - j:4 - j], u[:, j:S], ALU.mult, ALU.add,
                         reads=["xr", "cw", "u"], writes=["u"])
            self.cp("act", ub, u, reads=["u"], writes=["ub"])
            for gi, dst, nm in ((0, rr, "rr"), (1, ii_, "ii")):
                for g in range(4):
                    k = kk % 8
                    kk += 1
                    self.mm(self.pst[k][:, :], gwb[:, gi * 4 + n, :], ub[:, g * 512:(g + 1) * 512], True, True,
                            reads=["gwb", "ub"], writes=[self.psn(k)])
                    self.act(dst[:, g * 512:(g + 1) * 512], self.pst[k][:, :], AF.Sigmoid, bias=gb[:, gi, n:n + 1],
                             reads=[self.psn(k), "gb"], writes=[nm])
            self.act(aa, rr, AF.Exp, scale=clam[:, n:n + 1], reads=["rr", "clam"], writes=["aa"])
            self.tt("dve", rr, aa, aa, ALU.mult, reads=["aa"], writes=["rr"])
            self.ts("dve", rr, rr, -1.0, ALU.mult, reads=["rr"], writes=["rr"], s2=1.0, op1=ALU.add)
            self.act(rr, rr, AF.Sqrt, reads=["rr"], writes=["rr"])
            self.tt("pool", ii_, ii_, u, ALU.mult, reads=["ii", "u"], writes=["ii"])
            self.tt("dve", ii_, ii_, rr, ALU.mult, reads=["ii", "rr"], writes=["ii"])
            self.P.op("dve", lambda e_, o=u, a=aa, b=ii_: e_.tensor_tensor_scan(
                out=o, data0=a, data1=b, initial=0.0, op0=ALU.mult, op1=ALU.add),
                reads=["aa", "ii"], writes=["u"], cost=4.4)
            self.tt("pool", yo, u, szT, ALU.mult, reads=["u", "szT"], writes=["yo"])
            self.dma(self.y_fm[n * 128:(n + 1) * 128, :], yo, reads=["yo"], writes=["y_fm"], slot="yoC")
        self.run_pref()
        self.phase_end()

    def load_wout(self, L):
        wo = self.prm["w_out"][L]
        for nb in range(4):
            self.wload(wo[:, nb * 512:(nb + 1) * 512], 512, dst=self.xT, dcol0=nb * 512,
                       dres=self.allxt() + [("xTw", nb)])
        return True

    def out_proj(self, L, xsrc, dst):
        prm = self.prm
        self.take_pref(lambda: self.load_wout(L))
        self.run_pref()
        lng = self.carve([128, D], F32)
        lnb = self.carve([128, D], F32)
        self.dma(lng, prm["ln_g"][L:L + 1, :].broadcast_to([128, D]), reads=[], writes=["lng"], slot="lng")
        self.dma(lnb, prm["ln_b"][L:L + 1, :].broadcast_to([128, D]), reads=[], writes=["lnb"], slot="lnb")
        ya = [self.carve([128, D], BF16) for _ in range(2)]
        yT = [self.carve([128, 16, 128], BF16) for _ in range(2)]
        xres = [self.carve([128, D], F32) for _ in range(2)]
        tb = [self.carve([128, D], F32) for _ in range(2)]
        st6 = self.carve([128, 4, 6], F32)
        mv = self.carve([128, 2], F32)
        rstd = self.carve([128, 1], F32)
        yfv = self.y_fm.rearrange("(n f) t -> f n t", f=128)
        for t in range(NT):
            qs = slice(t * 128, (t + 1) * 128)
            i2 = t % 2
            self.dma(ya[i2], self.y_tm[qs, :], reads=["y_tm"], writes=[("ya", i2)], slot=("ya", i2))
            self.dma(yT[i2][:, 8:12, :], yfv[:, :, qs], reads=["y_fm"], writes=[("yT", i2)], slot=("yTf", i2))
            self.dma(xres[i2], xsrc[qs, :], reads=["xsrc"], writes=[("xres", i2)], slot=("xres", i2))
            for gi, fcs in enumerate(((0, 1, 2, 3, 4, 5, 6, 7), (12, 13, 14, 15))):
                kbk = 4 + gi
                pb = self.pst[kbk][:, :].bitcast(BF16)
                for j, fc in enumerate(fcs):
                    self.tr(pb[:, j * 128:(j + 1) * 128], ya[i2][:, fc * 128:(fc + 1) * 128], self.identb,
                            reads=[("ya", i2), "identb"], writes=[self.psn(kbk)])
                n_ = len(fcs)
                self.cp("act" if gi == 0 else "dve", yT[i2][:, fcs[0]:fcs[0] + n_, :],
                        pb[:, 0:n_ * 128].rearrange("p (c q) -> p c q", c=n_), reads=[self.psn(kbk)], writes=[("yT", i2)])
            for nb in range(4):
                k = nb
                for fc in range(16):
                    self.mm(self.pst[k][:, :], yT[i2][:, fc, :], self.xT[:, fc, nb * 512:(nb + 1) * 512], fc == 0, fc == 15,
                            reads=[("yT", i2), ("xTw", nb)], writes=[self.psn(k)])
                ns = slice(nb * 512, (nb + 1) * 512)
                self.stt(tb[i2][:, ns], xres[i2][:, ns], ALPHA, self.pst[k][:, :], ALU.mult, ALU.add,
                         reads=[("xres", i2), self.psn(k)], writes=[("tb", i2)])
                self.P.op("dve", lambda e_, o=st6[:, nb, :], i=tb[i2][:, ns]: e_.bn_stats(out=o, in_=i),
                          reads=[("tb", i2)], writes=["st6o"])
            self.P.op("dve", lambda e_, o=mv, i=st6.rearrange("p a b -> p (a b)"): e_.bn_aggr(out=o, in_=i),
                      reads=["st6o"], writes=["mvo"])
            self.ts("dve", rstd, mv[:, 1:2], 1e-5, ALU.add, reads=["mvo"], writes=["rstdo"])
            self.act(rstd, rstd, AF.Ln, reads=["rstdo"], writes=["rstdo"])
            self.act(rstd, rstd, AF.Exp, scale=-0.5, reads=["rstdo"], writes=["rstdo"])
            self.ts("dve", tb[i2], tb[i2], mv[:, 0:1], ALU.subtract, reads=[("tb", i2), "mvo", "rstdo"],
                    writes=[("tb", i2)], s2=rstd[:, 0:1], op1=ALU.mult)
            self.tt("pool", tb[i2], tb[i2], lng, ALU.mult, reads=[("tb", i2), "lng"], writes=[("tb", i2)])
            self.tt("pool", tb[i2], tb[i2], lnb, ALU.add, reads=[("tb", i2), "lnb"], writes=[("tb", i2)])
            self.dma(dst[qs, :], tb[i2], reads=[("tb", i2)], writes=["dst%d" % L], slot=("tbo", i2))
        self.phase_end()


def make_consts():
    bf = ml_dtypes.bfloat16
    c = {}
    c["identf"] = np.eye(128, dtype=np.float32)
    c["identb"] = np.eye(128, dtype=np.float32).astype(bf)
    half = 16
    inv_freq = np.power(np.float32(500000.0), -np.arange(half, dtype=np.float32) * np.float32(2.0 / 32))
    ang = np.arange(S, dtype=np.float32)[:, None] * inv_freq[None, :]
    cos = np.cos(ang).astype(np.float32).T
    sin = np.sin(ang).astype(np.float32).T
    c["cosT"] = np.concatenate([cos, cos], 0)
    c["sinT"] = np.concatenate([sin, sin], 0)
    rot = np.zeros((32, 32), np.float32)
    for m in range(16):
        rot[m + 16, m] = -1.0
        rot[m, m + 16] = 1.0
    c["rotP"] = rot
    n = np.arange(128)[:, None]
    q = np.arange(S)[None, :]
    c["cmpb"] = np.where((16 * n + 31 <= q) & (n < 127), 0.0, NEGB).astype(bf)
    esel = np.zeros((32, NT, 128), np.float32)
    for kt in range(NT):
        for k in range(128):
            esel[2 * kt + k // 64, kt, k] = -NEGB
    c["esel"] = esel.astype(bf)
    kl = np.arange(128)[:, None]
    ql = np.arange(128)[None, :]
    c["cbd"] = np.where(kl <= ql, 0.0, NEGB).astype(bf)
    c["cbw"] = np.where(kl > ql, 0.0, NEGB).astype(bf)
    cs = np.arange(128) * 16
    ss = np.arange(32) * 64
    ovl = ((cs[:, None] < ss[None, :] + 64) & (cs[:, None] + 32 > ss[None, :])).astype(np.float32)
    ovl[127, :] = 0.0
    o34 = np.zeros((128, 34), np.float32)
    o34[:, :32] = ovl
    o34[:127, 32] = 1.0
    c["ovl"] = o34.astype(bf)
    t = np.arange(S)
    cur = t // 64
    j = np.arange(32)[None, :]
    forced = (j == 0) | (j == cur[:, None]) | (j == cur[:, None] - 1)
    valid = j <= cur[:, None]
    m1 = (valid & ~forced).astype(np.float32)
    m2 = np.where(forced, 1e9, np.where(valid, 0.0, -1e30)).astype(np.float32)
    c["m1"] = np.ascontiguousarray(m1.reshape(NT, 128, 32).transpose(1, 0, 2))
    c["m2"] = np.ascontiguousarray(m2.reshape(NT, 128, 32).transpose(1, 0, 2))
    e8 = np.zeros((8, NT, 128), np.float32)
    for kt in range(NT):
        e8[kt // 2, kt, :] = -NEGB
    c["e8"] = e8.astype(bf)
    curb = (np.arange(NT) // 2)[:, None]
    jb = np.arange(8)[None, :]
    past = (jb < curb)
    c["pastb"] = np.ascontiguousarray(np.broadcast_to(np.where(past, 0.0, -1e30).astype(np.float32)[None], (128, NT, 8)))
    c["pastm"] = np.ascontiguousarray(np.broadcast_to(past.astype(np.float32)[None], (128, NT, 8)))
    c["triu"] = (kl <= ql).astype(np.float32)
    c["onesf"] = np.ones((128, 128), np.float32)
    c["caus01"] = (kl <= ql).astype(np.float32).astype(bf)
    return c


PARAM_NAMES = ["w_in", "nsa_cmp_w1", "nsa_cmp_w2", "nsa_cmp_pe", "mlstm_i_bias", "mlstm_f_bias",
               "mlstm_norm_g", "lru_conv_w", "lru_conv_b", "lru_gate_w", "lru_gate_b", "lru_lambda",
               "w_out", "ln_g", "ln_b"]
PARAM_SHAPES = {
    "w_in": [2, D, INW], "nsa_cmp_w1": [2, 2, 4096, 128], "nsa_cmp_w2": [2, 2, 128, 128],
    "nsa_cmp_pe": [2, 2, 32, 128], "mlstm_i_bias": [2, 4], "mlstm_f_bias": [2, 4],
    "mlstm_norm_g": [2, 512], "lru_conv_w": [2, 4, 512], "lru_conv_b": [2, 512],
    "lru_gate_w": [2, 2, 4, 128, 128], "lru_gate_b": [2, 2, 512], "lru_lambda": [2, 512],
    "w_out": [2, D, D], "ln_g": [2, D], "ln_b": [2, D],
}


def build(layers=(0, 1), groups="ABCD", do_out=True, dbg=False, consts=None):
    nc = bass.Bass("TRN2", target_bir_lowering=False)
    kb = KB(nc, dbg=dbg)
    kb.consts = {}
    x = nc.dram_tensor("x", [S, D], F32, kind="ExternalInput").ap()
    prm = {n: nc.dram_tensor(n, PARAM_SHAPES[n], F32, kind="ExternalInput").ap() for n in PARAM_NAMES}
    kb.cst = {}
    for n, a in consts.items():
        dt = BF16 if a.dtype == ml_dtypes.bfloat16 else F32
        kb.cst[n] = nc.dram_tensor("c_" + n, list(a.shape), dt, kind="ExternalInput").ap()
    out = nc.dram_tensor("out", [S, D], F32, kind="ExternalOutput").ap()
    okind = "ExternalOutput" if dbg else "Internal"
    kb.y_tm = nc.dram_tensor("y_tm", [S, D], BF16, kind=okind).ap()
    kb.y_fm = nc.dram_tensor("y_fm", [512, S], BF16, kind=okind).ap()
    x1 = nc.dram_tensor("x1s", [S, D], F32, kind="Internal").ap()
    if dbg:
        kb.dbg_ya = nc.dram_tensor("dbg_ya", [S, 512], F32, kind="ExternalOutput").ap()
    kb.w_in = [prm["w_in"][l] for l in range(2)]
    kb.w_cmp1 = [prm["nsa_cmp_w1"][l] for l in range(2)]
    kb.w_cmp2 = [prm["nsa_cmp_w2"][l] for l in range(2)]
    kb.w_pe = [prm["nsa_cmp_pe"][l] for l in range(2)]
    kb.prm = prm
    with kb.st:
        kb.setup()
        kb.dma(kb.identf, kb.cst["identf"], reads=[], writes=["identf"], slot="identf")
        kb.dma(kb.identb, kb.cst["identb"], reads=[], writes=["identb"], slot="identb")
        srcs = [x, x1]
        dsts = [x1, out]
        if (groups == "ABCD" and do_out and FLAG_PREF) or os.environ.get("K_FORCEPREF"):
            def mk(L):
                w = kb.w_in[L]
                return [lambda: kb.wload(w[:, C_NSA_Q:C_NSA_Q + 512], 512),
                        lambda: kb.wload(w[:, C_ML_QKV:C_ML_QKV + 512], 512),
                        lambda: kb.wload(w[:, C_LRU_X:C_LRU_X + 512], 512),
                        lambda: kb.wload(w[:, C_MB_QKV:C_MB_QKV + 512], 512),
                        lambda: kb.load_wout(L)]
            kb.pref_q = [(f if (FLAG_PREF >> i) & 1 else (lambda: None)) for L in layers for i, f in enumerate(mk(L))]
        if len(layers) == 1:
            srcs = {layers[0]: x}
            dsts = {layers[0]: out}
        for L in layers:
            kb.phase0(srcs[L])
            if "A" in groups:
                kb.group_nsa(L)
            if "B" in groups:
                kb.group_mlstm(L)
            if "C" in groups:
                kb.group_lru(L)
            if "D" in groups:
                kb.group_moba(L)
            if do_out:
                kb.out_proj(L, srcs[L], dsts[L])
        kb.P.emit()
    return nc


_CACHE = {}


def _get_nc(layers=(0, 1)):
    key = tuple(layers)
    if key not in _CACHE:
        consts = make_consts()
        _CACHE[key] = (build(layers=layers, consts=consts), consts)
    return _CACHE[key]


def kernel(**inputs):
    nc, consts = _get_nc((0, 1))
    x = np.ascontiguousarray(np.asarray(inputs["x"], dtype=np.float32))
    B = x.shape[0]
    base = {n: np.ascontiguousarray(np.asarray(inputs[n], dtype=np.float32)) for n in PARAM_NAMES}
    for n, a in consts.items():
        base["c_" + n] = a
    in_maps = []
    for b in range(B):
        m = dict(base)
        m["x"] = x[b]
        in_maps.append(m)
    res = run_bass_kernel_spmd(nc, in_maps, core_ids=list(range(B)))
    return np.stack([np.asarray(r["out"], dtype=np.float32) for r in res.results], axis=0)
```
